# Optimizing a Trainium2 kernel written in Bass

```python
import jax, jax.numpy as jnp
from jax import lax
import numpy as np


D_MODEL = 1024
BATCH = 4
SEQ = 8192
DEPTH = 2

PLE_DIM = 256
D_FF = 2816
EPS = 1e-6
MLA_HEADS = 8
MLA_NOPE = 64
MLA_ROPE = 32
MLA_V = 64
MLA_Q_RANK = 384
MLA_KV_RANK = 256
ROPE_BASE = 10000.0
Q_BLOCK = 128
GLA_HEADS = 4
GLA_DK = 64
GLA_DV = 128
GLA_GATE_RANK = 16
GLA_TAU = 16.0
GLA_CHUNK = 64
RG_WIDTH = 512
RG_BLOCKS = 8
RG_CONV = 4
RG_C = 8.0
S5_GROUP = 16
S5_GROUPS = 32
S5_STATE = 64
S5_WIDTH = S5_GROUP * S5_GROUPS

EVEN_MIX_WIDTH = MLA_HEADS * MLA_V + GLA_HEADS * GLA_DV
ODD_MIX_WIDTH = RG_WIDTH + S5_WIDTH
EVEN_IN_SPLITS = [MLA_Q_RANK, MLA_KV_RANK, MLA_ROPE,
                  GLA_HEADS * GLA_DK, GLA_HEADS * GLA_DK, GLA_HEADS * GLA_DV,
                  GLA_GATE_RANK, GLA_HEADS * GLA_DV]
EVEN_IN_WIDTH = sum(EVEN_IN_SPLITS)
ODD_IN_SPLITS = [RG_WIDTH, RG_WIDTH, S5_WIDTH]
ODD_IN_WIDTH = sum(ODD_IN_SPLITS)
N_EVEN = (DEPTH + 1) // 2
N_ODD = DEPTH // 2

kernel_name = 'hybrid_mla_gla_rglru_s5_macaron'


def rms_norm(x, g):
    x32 = x.astype(jnp.float32)
    y = x32 * lax.rsqrt(jnp.mean(x32 * x32, axis=-1, keepdims=True) + EPS)
    return y.astype(x.dtype) * g


def split_cols(y, sizes):
    offs = [sum(sizes[:n]) for n in range(1, len(sizes))]
    return jnp.split(y, offs, axis=-1)


def swiglu(x, w1, w3, w2):
    return (jax.nn.silu(x @ w1) * (x @ w3)) @ w2


def rope(x, positions):
    half = MLA_ROPE // 2
    inv = ROPE_BASE ** (-jnp.arange(half, dtype=jnp.float32) / half)
    ang = positions.astype(jnp.float32)[:, None] * inv[None, :]
    cos = jnp.cos(ang)[:, None, :]
    sin = jnp.sin(ang)[:, None, :]
    x1, x2 = x[..., :half], x[..., half:]
    return jnp.concatenate([x1 * cos - x2 * sin, x1 * sin + x2 * cos], axis=-1).astype(x.dtype)


def mla(c_q, c_kv, k_rope_in, q_norm, w_q_up, kv_norm, w_kv_up):
    B, S, _ = c_q.shape
    pos = jnp.arange(S)
    q = (rms_norm(c_q, q_norm) @ w_q_up).reshape(B, S, MLA_HEADS, MLA_NOPE + MLA_ROPE)
    scale = (MLA_NOPE + MLA_ROPE) ** -0.5
    q_nope = q[..., :MLA_NOPE] * scale
    q_rope = rope(q[..., MLA_NOPE:], pos) * scale
    kv = (rms_norm(c_kv, kv_norm) @ w_kv_up).reshape(B, S, MLA_HEADS, MLA_NOPE + MLA_V)
    k_nope, v = kv[..., :MLA_NOPE], kv[..., MLA_NOPE:]
    k_rope = rope(k_rope_in[:, :, None, :], pos)[:, :, 0, :]
    outs = []
    for blk in range(S // Q_BLOCK):
        s0 = blk * Q_BLOCK
        s1 = s0 + Q_BLOCK
        sc = (jnp.einsum('bqhd,bkhd->bhqk', q_nope[:, s0:s1], k_nope[:, :s1])
              + jnp.einsum('bqhr,bkr->bhqk', q_rope[:, s0:s1], k_rope[:, :s1]))
        mask = jnp.arange(s1)[None, :] <= jnp.arange(s0, s1)[:, None]
        sc = jnp.where(mask, sc.astype(jnp.float32), -jnp.inf)
        pr = jax.nn.softmax(sc, axis=-1).astype(v.dtype)
        outs.append(jnp.einsum('bhqk,bkhd->bqhd', pr, v[:, :s1]))
    o = jnp.concatenate(outs, axis=1)
    return o.reshape(B, S, MLA_HEADS * MLA_V)


def gla(q, k, v, g_low, r, w_gate_up, b_gate, out_norm):
    B, S, _ = q.shape
    H, dk, dv, C = GLA_HEADS, GLA_DK, GLA_DV, GLA_CHUNK
    N = S // C
    f32 = jnp.float32

    def heads(t, d):
        return t.astype(f32).reshape(B, N, C, H, d).transpose(0, 3, 1, 2, 4)

    qh = heads(q, dk) * (dk ** -0.5)
    kh = heads(k, dk)
    vh = heads(v, dv)
    g = jax.nn.log_sigmoid((g_low @ w_gate_up + b_gate).astype(f32)) / GLA_TAU
    b = jnp.cumsum(heads(g, dk), axis=3)
    b_last = b[:, :, :, -1:, :]
    q_e = qh * jnp.exp(b)
    k_e = kh * jnp.exp(-b)
    causal = jnp.tril(jnp.ones((C, C), dtype=bool))
    att = jnp.where(causal, jnp.einsum('bhncd,bhnjd->bhncj', q_e, k_e), 0.0)
    o_intra = jnp.einsum('bhncj,bhnje->bhnce', att, vh)
    k_end = kh * jnp.exp(b_last - b)
    chunk_kv = jnp.einsum('bhncd,bhnce->nbhde', k_end, vh)
    decay = jnp.exp(b_last[:, :, :, 0, :]).transpose(2, 0, 1, 3)

    def step(state, inp):
        dec, kv = inp
        return dec[..., None] * state + kv, state

    _, states = lax.scan(step, jnp.zeros((B, H, dk, dv), f32), (decay, chunk_kv))
    o_inter = jnp.einsum('bhncd,nbhde->bhnce', q_e, states)
    o = (o_intra + o_inter).transpose(0, 2, 3, 1, 4).reshape(B, S, H, dv)
    o = rms_norm(o, out_norm).reshape(B, S, H * dv) * jax.nn.silu(r.astype(f32))
    return o.astype(q.dtype)


def even_mixer(xn, w_in, q_norm, w_q_up, kv_norm, w_kv_up, w_gate_up, b_gate, out_norm, w_out):
    c_q, c_kv, k_r, gq, gk, gv, g_low, g_r = split_cols(xn @ w_in, EVEN_IN_SPLITS)
    y_a = mla(c_q, c_kv, k_r, q_norm, w_q_up, kv_norm, w_kv_up)
    y_b = gla(gq, gk, gv, g_low, g_r, w_gate_up, b_gate, out_norm)
    return jnp.concatenate([y_a, y_b], axis=-1) @ w_out


def rg_lru_branch(x_gate, x_in, conv_w, conv_b, w_a, b_a, w_i, b_i, lam):
    B, S, W = x_in.shape
    f32 = jnp.float32
    xc = lax.conv_general_dilated(x_in, conv_w[:, None, :], window_strides=(1,),
                                  padding=[(RG_CONV - 1, 0)],
                                  dimension_numbers=('NWC', 'WIO', 'NWC'),
                                  feature_group_count=W) + conv_b
    xb = xc.reshape(B, S, RG_BLOCKS, W // RG_BLOCKS)
    r = jax.nn.sigmoid(jnp.einsum('bshi,hij->bshj', xb, w_a).reshape(B, S, W) + b_a)
    i = jax.nn.sigmoid(jnp.einsum('bshi,hij->bshj', xb, w_i).reshape(B, S, W) + b_i)
    log_a = (-RG_C * jax.nn.softplus(-lam) * r).astype(f32)
    a = jnp.exp(log_a)
    bx = jnp.sqrt(-jnp.expm1(2.0 * log_a)) * (i * xc).astype(f32)

    def comb(e1, e2):
        a1, b1 = e1
        a2, b2 = e2
        return a1 * a2, a2 * b1 + b2

    _, h = lax.associative_scan(comb, (a, bx), axis=1)
    return h.astype(x_in.dtype) * jax.nn.gelu(x_gate)


def s5_branch(u, a_re, a_im, log_dt, b_re, b_im, c_re, c_im, d, w_glu, b_glu):
    B, S, _ = u.shape
    f32 = jnp.float32
    ug = u.astype(f32).reshape(B, S, S5_GROUPS, S5_GROUP)
    dt = jnp.exp(log_dt.astype(f32))[:, None]
    lr, li = a_re.astype(f32), a_im.astype(f32)
    mag = jnp.exp(lr * dt)
    ab_re = mag * jnp.cos(li * dt)
    ab_im = mag * jnp.sin(li * dt)
    den = lr * lr + li * li
    nr, ni = ab_re - 1.0, ab_im
    coef_re = (nr * lr + ni * li) / den
    coef_im = (ni * lr - nr * li) / den
    bb_re = coef_re[..., None] * b_re - coef_im[..., None] * b_im
    bb_im = coef_re[..., None] * b_im + coef_im[..., None] * b_re
    bu_re = jnp.einsum('bsgc,gpc->bsgp', ug, bb_re)
    bu_im = jnp.einsum('bsgc,gpc->bsgp', ug, bb_im)
    at_re = jnp.broadcast_to(ab_re, (1, S) + ab_re.shape)
    at_im = jnp.broadcast_to(ab_im, (1, S) + ab_im.shape)

    def comb(e1, e2):
        a1r, a1i, b1r, b1i = e1
        a2r, a2i, b2r, b2i = e2
        return (a2r * a1r - a2i * a1i, a2r * a1i + a2i * a1r,
                a2r * b1r - a2i * b1i + b2r, a2r * b1i + a2i * b1r + b2i)

    _, _, x_re, x_im = lax.associative_scan(comb, (at_re, at_im, bu_re, bu_im), axis=1)
    y = (jnp.einsum('bsgp,gcp->bsgc', x_re, c_re) - jnp.einsum('bsgp,gcp->bsgc', x_im, c_im)
         + d * ug)
    y = jax.nn.gelu(y.reshape(B, S, S5_WIDTH))
    y = y * jax.nn.sigmoid(y @ w_glu + b_glu)
    return y.astype(u.dtype)


def odd_mixer(xn, w_in, conv_w, conv_b, w_a, b_a, w_i, b_i, lam,
              a_re, a_im, log_dt, b_re, b_im, c_re, c_im, d, w_glu, b_glu, w_out):
    x_gate, x_rg, u = split_cols(xn @ w_in, ODD_IN_SPLITS)
    y_c = rg_lru_branch(x_gate, x_rg, conv_w, conv_b, w_a, b_a, w_i, b_i, lam)
    y_d = s5_branch(u, a_re, a_im, log_dt, b_re, b_im, c_re, c_im, d, w_glu, b_glu)
    return jnp.concatenate([y_c, y_d], axis=-1) @ w_out


def setup_inputs(seed: int = 0) -> dict:
    key = jax.random.key(seed)
    ks = iter(jax.random.split(key, 64))
    f32 = jnp.float32
    D, L, NE, NO = D_MODEL, DEPTH, N_EVEN, N_ODD
    G, P = S5_GROUPS, S5_STATE

    def nrm(shape, fan_in):
        return jax.random.normal(next(ks), shape, f32) * (fan_in ** -0.5)

    def gain(shape):
        return 1.0 + 0.01 * jax.random.normal(next(ks), shape, f32)

    def small(shape):
        return 0.01 * jax.random.normal(next(ks), shape, f32)

    x = jax.random.normal(next(ks), (BATCH, SEQ, D), f32)
    p = jax.random.normal(next(ks), (DEPTH, BATCH, SEQ, PLE_DIM), f32)
    u_lam = jax.random.uniform(next(ks), (NO, RG_WIDTH), f32, 0.9, 0.999)
    a_lam = u_lam ** (1.0 / RG_C)
    rg_lambda = jnp.log(a_lam) - jnp.log1p(-a_lam)
    s5_a_re = -0.5 + small((NO, G, P))
    s5_a_im = jnp.pi * jnp.arange(P, dtype=f32) + small((NO, G, P))
    s5_log_dt = jnp.log(jax.random.uniform(next(ks), (NO, G), f32, 0.001, 0.1))
    return {
        'x': x, 'p': p,
        'ffn_a_norm': gain((L, D)), 'ffn_a_w1': nrm((L, D, D_FF), D),
        'ffn_a_w3': nrm((L, D, D_FF), D), 'ffn_a_w2': nrm((L, D_FF, D), D_FF),
        'mix_norm': gain((L, D)),
        'ffn_b_norm': gain((L, D)), 'ffn_b_w1': nrm((L, D, D_FF), D),
        'ffn_b_w3': nrm((L, D, D_FF), D), 'ffn_b_w2': nrm((L, D_FF, D), D_FF),
        'ple_norm': gain((L, D)), 'ple_w_gate': nrm((L, D, D), D),
        'ple_w_up': nrm((L, PLE_DIM, D), PLE_DIM),
        'ev_w_in': nrm((NE, D, EVEN_IN_WIDTH), D),
        'mla_q_norm': gain((NE, MLA_Q_RANK)),
        'mla_w_q_up': nrm((NE, MLA_Q_RANK, MLA_HEADS * (MLA_NOPE + MLA_ROPE)), MLA_Q_RANK),
        'mla_kv_norm': gain((NE, MLA_KV_RANK)),
        'mla_w_kv_up': nrm((NE, MLA_KV_RANK, MLA_HEADS * (MLA_NOPE + MLA_V)), MLA_KV_RANK),
        'gla_w_gate_up': nrm((NE, GLA_GATE_RANK, GLA_HEADS * GLA_DK), GLA_GATE_RANK),
        'gla_b_gate': small((NE, GLA_HEADS * GLA_DK)),
        'gla_out_norm': gain((NE, GLA_DV)),
        'ev_w_out': nrm((NE, EVEN_MIX_WIDTH, D), EVEN_MIX_WIDTH),
        'od_w_in': nrm((NO, D, ODD_IN_WIDTH), D),
        'rg_conv_w': nrm((NO, RG_CONV, RG_WIDTH), RG_CONV),
        'rg_conv_b': small((NO, RG_WIDTH)),
        'rg_w_a': nrm((NO, RG_BLOCKS, RG_WIDTH // RG_BLOCKS, RG_WIDTH // RG_BLOCKS), RG_WIDTH // RG_BLOCKS),
        'rg_b_a': small((NO, RG_WIDTH)),
        'rg_w_i': nrm((NO, RG_BLOCKS, RG_WIDTH // RG_BLOCKS, RG_WIDTH // RG_BLOCKS), RG_WIDTH // RG_BLOCKS),
        'rg_b_i': small((NO, RG_WIDTH)),
        'rg_lambda': rg_lambda,
        's5_a_re': s5_a_re, 's5_a_im': s5_a_im, 's5_log_dt': s5_log_dt,
        's5_b_re': nrm((NO, G, P, S5_GROUP), 2 * S5_GROUP),
        's5_b_im': nrm((NO, G, P, S5_GROUP), 2 * S5_GROUP),
        's5_c_re': nrm((NO, G, S5_GROUP, P), P),
        's5_c_im': nrm((NO, G, S5_GROUP, P), P),
        's5_d': jax.random.normal(next(ks), (NO, G, S5_GROUP), f32),
        's5_w_glu': nrm((NO, S5_WIDTH, S5_WIDTH), S5_WIDTH),
        's5_b_glu': small((NO, S5_WIDTH)),
        'od_w_out': nrm((NO, ODD_MIX_WIDTH, D), ODD_MIX_WIDTH),
        'final_norm': gain((D,)),
    }


def reference(x, p, ffn_a_norm, ffn_a_w1, ffn_a_w3, ffn_a_w2, mix_norm,
              ffn_b_norm, ffn_b_w1, ffn_b_w3, ffn_b_w2, ple_norm, ple_w_gate, ple_w_up,
              ev_w_in, mla_q_norm, mla_w_q_up, mla_kv_norm, mla_w_kv_up,
              gla_w_gate_up, gla_b_gate, gla_out_norm, ev_w_out,
              od_w_in, rg_conv_w, rg_conv_b, rg_w_a, rg_b_a, rg_w_i, rg_b_i, rg_lambda,
              s5_a_re, s5_a_im, s5_log_dt, s5_b_re, s5_b_im, s5_c_re, s5_c_im, s5_d,
              s5_w_glu, s5_b_glu, od_w_out, final_norm):
    h = x
    for i in range(DEPTH):
        h = h + 0.5 * swiglu(rms_norm(h, ffn_a_norm[i]), ffn_a_w1[i], ffn_a_w3[i], ffn_a_w2[i])
        hn = rms_norm(h, mix_norm[i])
        j = i // 2
        if i % 2 == 0:
            h = h + even_mixer(hn, ev_w_in[j], mla_q_norm[j], mla_w_q_up[j], mla_kv_norm[j],
                               mla_w_kv_up[j], gla_w_gate_up[j], gla_b_gate[j],
                               gla_out_norm[j], ev_w_out[j])
        else:
            h = h + odd_mixer(hn, od_w_in[j], rg_conv_w[j], rg_conv_b[j], rg_w_a[j], rg_b_a[j],
                              rg_w_i[j], rg_b_i[j], rg_lambda[j], s5_a_re[j], s5_a_im[j],
                              s5_log_dt[j], s5_b_re[j], s5_b_im[j], s5_c_re[j], s5_c_im[j],
                              s5_d[j], s5_w_glu[j], s5_b_glu[j], od_w_out[j])
        h = h + 0.5 * swiglu(rms_norm(h, ffn_b_norm[i]), ffn_b_w1[i], ffn_b_w3[i], ffn_b_w2[i])
        h = h + (p[i] @ ple_w_up[i]) * jax.nn.sigmoid(rms_norm(h, ple_norm[i]) @ ple_w_gate[i])
    return rms_norm(h, final_norm)
```

```python
import numpy as np
import ml_dtypes
import concourse.bass as bass
import concourse.mybir as mybir
from concourse.bass_utils import run_bass_kernel_spmd

F32 = mybir.dt.float32
BF16 = mybir.dt.bfloat16
I32 = mybir.dt.int32
AF = mybir.ActivationFunctionType
ALU = mybir.AluOpType
NPBF = ml_dtypes.bfloat16

NCORES = 8
D = 1024
DFF = 2816
SEQ = 8192
TOK = 4096
TT = 512
KC = D // 128
FC = DFF // 128
EPS = 1e-6


class Res:
    __slots__ = ("name", "writer", "readers")

    def __init__(self, name):
        self.name = name
        self.writer = None
        self.readers = []


class Sched:
    ENGS = ("pe", "act", "dve", "pool", "sp")

    def __init__(self, nc, n_dma_sems=40):
        self.nc = nc
        self.q = {e: [] for e in self.ENGS}
        self.sem = {e: nc.alloc_semaphore("prog_" + e) for e in self.ENGS}
        self.cnt = {e: 0 for e in self.ENGS}
        self.waited = {e: {} for e in self.ENGS}
        self.dsems = [nc.alloc_semaphore(f"dma{i}") for i in range(n_dma_sems)]
        self.dcnt = [0] * n_dma_sems
        self.dnext = 0
        self.semname = {}
        self.all_dma_tokens = []

    def _need(self, eng, waits, tok):
        if tok is None:
            return
        s, v = tok
        key = id(s)
        if self.waited[eng].get(key, 0) >= v:
            return
        if key in waits and waits[key][1] >= v:
            return
        waits[key] = (s, v)

    def _deps(self, eng, reads, writes):
        waits = {}
        for r in reads:
            self._need(eng, waits, r.writer)
        for w in writes:
            self._need(eng, waits, w.writer)
            for t in w.readers:
                self._need(eng, waits, t)
        return waits

    def _commit(self, eng, waits, tok, reads, writes):
        for s, v in waits.values():
            self.waited[eng][id(s)] = max(self.waited[eng].get(id(s), 0), v)
        for r in reads:
            r.readers.append(tok)
        for w in writes:
            w.writer = tok
            w.readers = []

    def op(self, eng, fn, reads=(), writes=()):
        waits = self._deps(eng, reads, writes)
        self.cnt[eng] += 1
        tok = (self.sem[eng], self.cnt[eng])
        self._commit(eng, waits, tok, reads, writes)
        self.q[eng].append((list(waits.values()), fn, (self.sem[eng], 1)))
        return tok

    def raw(self, eng, fn):
        self.q[eng].append(([], fn, None))

    def dma(self, eng, fn, reads=(), writes=()):
        waits = self._deps(eng, reads, writes)
        i = self.dnext
        self.dnext = (self.dnext + 1) % len(self.dsems)
        s = self.dsems[i]
        if self.dcnt[i] > 0:
            self._need(eng, waits, (s, self.dcnt[i]))
        self.dcnt[i] += 16
        tok = (s, self.dcnt[i])
        self._commit(eng, waits, tok, reads, writes)
        self.q[eng].append((list(waits.values()), fn, (s, 16)))
        return tok

    def barrier(self):
        waits = []
        for i, s in enumerate(self.dsems):
            if self.dcnt[i] > 0:
                waits.append((s, self.dcnt[i]))
        for e in self.ENGS:
            if self.cnt[e] > 0:
                waits.append((self.sem[e], self.cnt[e]))
        for e in self.ENGS:
            self.q[e].append(([w for w in waits if w[0] is not self.sem[e]], None, None))
            for s_, v_ in waits:
                self.waited[e][id(s_)] = max(self.waited[e].get(id(s_), 0), v_)

    def finish(self, eng="sp"):
        waits = []
        for i, s in enumerate(self.dsems):
            if self.dcnt[i] > 0:
                waits.append((s, self.dcnt[i]))
        for e in self.ENGS:
            if self.cnt[e] > 0 and e != eng:
                waits.append((self.sem[e], self.cnt[e]))
        self.q[eng].append((waits, None, None))

    def emit(self):
        nc = self.nc
        q = self.q

        def replay(e, eng):
            for waits, fn, inc in q[e]:
                for s, v in waits:
                    eng.wait_ge(s, v)
                if fn is None:
                    continue
                ins = fn(eng)
                if inc is not None:
                    ins.then_inc(inc[0], inc[1])

        with nc.Block() as block:
            @block.tensor
            def _(eng):
                replay("pe", eng)

            @block.scalar
            def _(eng):
                replay("act", eng)

            @block.vector
            def _(eng):
                replay("dve", eng)

            @block.gpsimd
            def _(eng):
                replay("pool", eng)

            @block.sync
            def _(eng):
                replay("sp", eng)


class Ctx:
    def __init__(self, bf_bank=False):
        self.nc = bass.Bass("TRN2", target_bir_lowering=False)
        self.S = Sched(self.nc)
        self.n = 0
        self.psum = []
        for i in range(8):
            if bf_bank and i == 7:
                t = self.nc.alloc_psum_tensor(f"ps{i}", [128, 1024], BF16)
            else:
                t = self.nc.alloc_psum_tensor(f"ps{i}", [128, 512], F32)
            self.psum.append((t, Res(f"ps{i}")))
        self.ps_next = 0

    def name(self, p):
        self.n += 1
        return f"{p}_{self.n}"

    def sb(self, shape, dtype, name="t"):
        t = self.nc.alloc_sbuf_tensor(self.name(name), list(shape), dtype)
        return t, Res(name)

    def din(self, name, shape, dtype):
        return self.nc.dram_tensor(name, list(shape), dtype, kind="ExternalInput").ap()

    def dout(self, name, shape, dtype):
        return self.nc.dram_tensor(name, list(shape), dtype, kind="ExternalOutput").ap()

    def bank(self, lo=0, hi=8):
        i = lo + (self.ps_next % (hi - lo))
        self.ps_next += 1
        return self.psum[i]


def mm_group(S, out_ap, out_res, pairs, reads, start=True, stop=True):
    n = len(pairs)
    tok = None
    for i, (l, r) in enumerate(pairs):
        st = start and i == 0
        sp = stop and i == n - 1
        fn = (lambda e, l=l, r=r, st=st, sp=sp: e.matmul(out_ap, l, r, start=st, stop=sp))
        if i == 0 and n > 1:
            w = S._deps("pe", reads, [out_res])
            for s_, v_ in w.values():
                S.waited["pe"][id(s_)] = max(S.waited["pe"].get(id(s_), 0), v_)
            S.q["pe"].append((list(w.values()), fn, None))
        elif i == n - 1:
            tok = S.op("pe", fn, reads=reads, writes=[out_res])
        else:
            S.raw("pe", fn)
    return tok


class Consts:
    def __init__(self, cx):
        S = cx.S
        self.ones_bf, self.r_ones = cx.sb([128, 128], BF16, "ones")
        S.op("pool", lambda e: e.memset(self.ones_bf[:, :], 1.0), writes=[self.r_ones])
        self.ident, self.r_ident = cx.sb([128, 128], F32, "ident")
        S.op("pool", lambda e: e.memset(self.ident[:, :], 0.0), writes=[self.r_ident])
        S.op("pool", lambda e: e.affine_select(out=self.ident[:, :], in_=self.ident[:, :],
                                               pattern=[[1, 128]], compare_op=ALU.not_equal,
                                               fill=1.0, base=0, channel_multiplier=-1),
             reads=[self.r_ident], writes=[self.r_ident])


def load_vec_cols(cx, dram_vec_ap, n, name):
    t, r = cx.sb([128, n], F32, name)
    cx.S.dma("sp", lambda e: e.dma_start(out=t[:, :], in_=dram_vec_ap.rearrange("(c p) -> p c", p=128),
                                         allow_slow_non_contiguous=True), writes=[r])
    return t, r


def rmsnorm_T(cx, C, hT, r_h, nch, g, r_g, outT, r_out, nfeat, scratch, width=TT):
    S = cx.S
    sq, r_sq, rstd, r_rstd = scratch
    bank, r_bank = cx.bank(4, 8)
    pairs = []
    for c in range(nch):
        S.op("act", lambda e, c=c: e.activation(out=sq[:, c, :width], in_=hT[:, c, :width], func=AF.Square),
             reads=[r_h], writes=[r_sq])
        pairs.append((C.ones_bf[:, :], sq[:, c, :width]))
    mm_group(S, bank[:, :width], r_bank, pairs, reads=[r_sq, C.r_ones])
    S.op("act", lambda e: e.activation(out=rstd[:, :width], in_=bank[:, :width], func=AF.Sqrt,
                                       scale=1.0 / nfeat, bias=EPS),
         reads=[r_bank], writes=[r_rstd])
    S.op("dve", lambda e: e.reciprocal(out=rstd[:, :width], in_=rstd[:, :width]),
         reads=[r_rstd], writes=[r_rstd])
    for c in range(nch):
        S.op("dve", lambda e, c=c: e.scalar_tensor_tensor(out=outT[:, c, :width], in0=hT[:, c, :width],
                                                          scalar=g[:, c:c + 1], in1=rstd[:, :width],
                                                          op0=ALU.mult, op1=ALU.mult),
             reads=[r_h, r_g, r_rstd], writes=[r_out])


def pe_group(S, fns, reads, out_res):
    n = len(fns)
    tok = None
    for i, fn in enumerate(fns):
        if i == n - 1:
            tok = S.op("pe", fn, reads=reads, writes=[out_res])
        elif i == 0:
            w = S._deps("pe", reads, [out_res])
            for s_, v_ in w.values():
                S.waited["pe"][id(s_)] = max(S.waited["pe"].get(id(s_), 0), v_)
            S.q["pe"].append((list(w.values()), fn, None))
        else:
            S.raw("pe", fn)
    return tok


class WStream:
    def __init__(self, cx, kch, width, nbuf, name):
        self.slabs = [cx.sb([128, kch, width], BF16, name) for _ in range(nbuf)]
        self.i = 0

    def load(self, cx, w_ap, col0, width, eng="sp"):
        t, r = self.slabs[self.i % len(self.slabs)]
        self.i += 1
        kch = w_ap.shape[0] // 128
        src = w_ap.rearrange("(c p) f -> p c f", p=128)[:, :, col0:col0 + width]
        cx.S.dma(eng, lambda e: e.dma_start(out=t[:, :kch, :width], in_=src), writes=[r])
        return t, r


class RowState:
    def __init__(self, cx):
        self.hT, self.r_h = cx.sb([128, KC, TT], F32, "hT")
        self.xn, self.r_xn = cx.sb([128, KC, TT], BF16, "xn")
        self.sq, self.r_sq = cx.sb([128, KC, TT], BF16, "sq")
        self.rstd, self.r_rstd = cx.sb([128, TT], F32, "rstd")
        self.gT, _ = cx.sb([128, FC, TT], BF16, "gT")
        self.r_g = [Res(f"g{j}") for j in range(FC)]
        self.sl = [cx.sb([128, TT], F32, "sl") for _ in range(2)]
        self.sli = 0
        self.ws13 = WStream(cx, KC, 256, 4, "ws13")
        self.ws2 = WStream(cx, FC, 256, 2, "ws2")

    def scratch(self):
        return (self.sq, self.r_sq, self.rstd, self.r_rstd)


def ffn_T(cx, C, st, w1, w3, w2, g, r_g):
    S = cx.S
    rmsnorm_T(cx, C, st.hT, st.r_h, KC, g, r_g, st.xn, st.r_xn, D, st.scratch())
    for jb in range(FC // 2):
        a, ra = st.ws13.load(cx, w1, jb * 256, 256)
        b, rb = st.ws13.load(cx, w3, jb * 256, 256)
        for jj in range(2):
            j = jb * 2 + jj
            pA, rA = cx.bank(0, 4)
            pB, rB = cx.bank(0, 4)
            mm_group(S, pA[:, :], rA, [(a[:, k, jj * 128:(jj + 1) * 128], st.xn[:, k, :]) for k in range(KC)],
                     reads=[ra, st.r_xn])
            mm_group(S, pB[:, :], rB, [(b[:, k, jj * 128:(jj + 1) * 128], st.xn[:, k, :]) for k in range(KC)],
                     reads=[rb, st.r_xn])
            sl, r_sl = st.sl[st.sli % 2]
            st.sli += 1
            S.op("act", lambda e, sl=sl, pA=pA: e.activation(out=sl[:, :], in_=pA[:, :], func=AF.Silu),
                 reads=[rA], writes=[r_sl])
            S.op("dve", lambda e, sl=sl, pB=pB, j=j: e.tensor_tensor(out=st.gT[:, j, :], in0=pB[:, :], in1=sl[:, :],
                                                                     op=ALU.mult),
                 reads=[rB, r_sl], writes=[st.r_g[j]])
    for mb in range(KC // 2):
        c, rc = st.ws2.load(cx, w2, mb * 256, 256)
        for mm in range(2):
            m = mb * 2 + mm
            pO, rO = cx.bank(4, 8)
            mm_group(S, pO[:, :], rO, [(c[:, j, mm * 128:(mm + 1) * 128], st.gT[:, j, :]) for j in range(FC)],
                     reads=[rc] + st.r_g)
            S.op("dve", lambda e, pO=pO, m=m: e.scalar_tensor_tensor(out=st.hT[:, m, :], in0=pO[:, :], scalar=0.5,
                                                                     in1=st.hT[:, m, :], op0=ALU.mult, op1=ALU.add),
                 reads=[rO, st.r_h], writes=[st.r_h])


def load_tokmajor_T(cx, C, src_ap, tok0, nfeat_ch, xin, r_xin, dstT, r_dst, dst_is_bf16=False):
    S = cx.S
    S.dma("sp", lambda e: e.dma_start(out=xin[:, :, :nfeat_ch * 128],
                                      in_=src_ap[tok0:tok0 + TT, :].rearrange("(b p) d -> p b d", p=128)),
          writes=[r_xin])
    for c in range(nfeat_ch):
        bank, rb = cx.bank(4, 8)
        fns = [(lambda e, b=b, c=c, bank=bank: e.transpose(out=bank[:, b * 128:(b + 1) * 128],
                                                             in_=xin[:, b, c * 128:(c + 1) * 128],
                                                             identity=C.ident[:, :])) for b in range(TT // 128)]
        pe_group(S, fns, reads=[r_xin, C.r_ident], out_res=rb)
        S.op("act", lambda e, c=c, bank=bank: e.activation(out=dstT[:, c, :], in_=bank[:, :], func=AF.Copy),
             reads=[rb], writes=[r_dst])


def store_T(cx, dst_ap, tok0, srcT, r_src, nch, width=TT, eng="sp"):
    cx.S.dma(eng, lambda e: e.dma_start(out=dst_ap.rearrange("(c p) t -> p c t", p=128)[:, :, tok0:tok0 + width],
                                        in_=srcT[:, :nch, :width]), reads=[r_src])


def load_T(cx, src_ap, tok0, dstT, r_dst, nch, width=TT, eng="sp"):
    cx.S.dma(eng, lambda e: e.dma_start(out=dstT[:, :nch, :width],
                                        in_=src_ap.rearrange("(c p) t -> p c t", p=128)[:, :, tok0:tok0 + width]),
             writes=[r_dst])


def build_cast(sizes):
    cx = Ctx()
    S = cx.S
    for i, n in enumerate(sizes):
        rows = n // 512
        src = cx.din(f"w{i}", [rows, 512], F32)
        dst = cx.dout(f"o{i}", [rows, 512], BF16)
        step = 512
        for r0 in range(0, rows, step):
            r1 = min(rows, r0 + step)
            S.dma("pool", lambda e, r0=r0, r1=r1, src=src, dst=dst: e.dma_start(out=dst[r0:r1, :], in_=src[r0:r1, :]))
    S.finish("pool")
    S.emit()
    return cx.nc


def build_L1(ntiles=TOK // TT):
    cx = Ctx()
    S = cx.S
    x = cx.din("x", [TOK, D], F32)
    w1 = cx.din("w1", [D, DFF], BF16)
    w3 = cx.din("w3", [D, DFF], BF16)
    w2 = cx.din("w2", [DFF, D], BF16)
    gA = cx.din("gA", [D], F32)
    gM = cx.din("gM", [D], F32)
    hT_o = cx.dout("hT", [D, TOK], F32)
    xn_o = cx.dout("xnT", [D, TOK], BF16)
    C = Consts(cx)
    st = RowState(cx)
    xin, r_xin = cx.sb([128, TT // 128, D], F32, "xin")
    xn2, r_xn2 = cx.sb([128, KC, TT], BF16, "xn2")
    gA_t, r_gA = load_vec_cols(cx, gA, KC, "gA")
    gM_t, r_gM = load_vec_cols(cx, gM, KC, "gM")
    for t in range(ntiles):
        tok0 = t * TT
        load_tokmajor_T(cx, C, x, tok0, KC, xin, r_xin, st.hT, st.r_h)
        ffn_T(cx, C, st, w1, w3, w2, gA_t, r_gA)
        rmsnorm_T(cx, C, st.hT, st.r_h, KC, gM_t, r_gM, xn2, r_xn2, D, st.scratch())
        store_T(cx, xn_o, tok0, xn2, r_xn2, KC)
        store_T(cx, hT_o, tok0, st.hT, st.r_h, KC)
    S.finish("sp")
    S.emit()
    return cx.nc


def outproj_T(cx, st, w_out, ysrc, r_ysrc):
    S = cx.S
    for mb in range(KC // 2):
        a, ra = st.ws13.load(cx, w_out, mb * 256, 256)
        for mm in range(2):
            m = mb * 2 + mm
            pO, rO = cx.bank(4, 8)
            mm_group(S, pO[:, :], rO, [(a[:, k, mm * 128:(mm + 1) * 128], ysrc[k]) for k in range(KC)],
                     reads=[ra] + r_ysrc)
            S.op("dve", lambda e, pO=pO, m=m: e.tensor_tensor(out=st.hT[:, m, :], in0=pO[:, :], in1=st.hT[:, m, :],
                                                              op=ALU.add),
                 reads=[rO, st.r_h], writes=[st.r_h])


def ple_T(cx, C, st, w_gate, wup_t, r_wup, pT, r_pT, g, r_g):
    S = cx.S
    rmsnorm_T(cx, C, st.hT, st.r_h, KC, g, r_g, st.xn, st.r_xn, D, st.scratch())
    for mb in range(KC // 2):
        a, ra = st.ws13.load(cx, w_gate, mb * 256, 256)
        for mm in range(2):
            m = mb * 2 + mm
            pG, rG = cx.bank(4, 8)
            pU, rU = cx.bank(4, 8)
            mm_group(S, pG[:, :], rG, [(a[:, k, mm * 128:(mm + 1) * 128], st.xn[:, k, :]) for k in range(KC)],
                     reads=[ra, st.r_xn])
            mm_group(S, pU[:, :], rU, [(wup_t[:, k, m * 128:(m + 1) * 128], pT[:, k, :]) for k in range(2)],
                     reads=[r_wup, r_pT])
            sl, r_sl = st.sl[st.sli % 2]
            st.sli += 1
            S.op("act", lambda e, sl=sl, pG=pG: e.activation(out=sl[:, :], in_=pG[:, :], func=AF.Sigmoid),
                 reads=[rG], writes=[r_sl])
            S.op("dve", lambda e, sl=sl, pU=pU: e.tensor_tensor(out=sl[:, :], in0=pU[:, :], in1=sl[:, :], op=ALU.mult),
                 reads=[rU, r_sl], writes=[r_sl])
            S.op("dve", lambda e, sl=sl, m=m: e.tensor_tensor(out=st.hT[:, m, :], in0=sl[:, :], in1=st.hT[:, m, :],
                                                              op=ALU.add),
                 reads=[r_sl, st.r_h], writes=[st.r_h])


def load_resident(cx, w_ap, name):
    K_, M_ = w_ap.shape
    t, r = cx.sb([128, K_ // 128, M_], BF16, name)
    cx.S.dma("sp", lambda e: e.dma_start(out=t[:, :, :], in_=w_ap.rearrange("(c p) f -> p c f", p=128)), writes=[r])
    return t, r


def build_row(last, ntiles=TOK // TT):
    cx = Ctx()
    S = cx.S
    hT_i = cx.din("hT_in", [D, TOK], F32)
    yT_i = cx.din("yT_in", [D, TOK], BF16)
    p_i = cx.din("p", [TOK, 256], F32)
    w_out = cx.din("w_out", [D, D], BF16)
    w1 = cx.din("w1", [D, DFF], BF16)
    w3 = cx.din("w3", [D, DFF], BF16)
    w2 = cx.din("w2", [DFF, D], BF16)
    gB = cx.din("gB", [D], F32)
    gP = cx.din("gP", [D], F32)
    w_gate = cx.din("w_gate", [D, D], BF16)
    w_up = cx.din("w_up", [256, D], BF16)
    if last:
        w_glu = cx.din("w_glu", [512, 512], BF16)
        b_glu = cx.din("b_glu", [512], F32)
        gF = cx.din("gF", [D], F32)
        out_o = cx.dout("out", [TOK, D], F32)
    else:
        n1 = cx.din("n_w1", [D, DFF], BF16)
        n3 = cx.din("n_w3", [D, DFF], BF16)
        n2 = cx.din("n_w2", [DFF, D], BF16)
        gA = cx.din("gA", [D], F32)
        gM = cx.din("gM", [D], F32)
        hT_o = cx.dout("hT", [D, TOK], F32)
        xn_o = cx.dout("xnT", [D, TOK], BF16)
    C = Consts(cx)
    st = RowState(cx)
    xin, r_xin = cx.sb([128, TT // 128, D], F32, "xin")
    yin, r_yin = cx.sb([128, KC, TT], BF16, "yin")
    pT, r_pT = cx.sb([128, 2, TT], BF16, "pT")
    xn2, r_xn2 = cx.sb([128, KC, TT], F32 if last else BF16, "xn2")
    wup_t, r_wup = load_resident(cx, w_up, "wup")
    gB_t, r_gB = load_vec_cols(cx, gB, KC, "gB")
    gP_t, r_gP = load_vec_cols(cx, gP, KC, "gP")
    if last:
        wglu_t, r_wglu = load_resident(cx, w_glu, "wglu")
        bglu_t, r_bglu = load_vec_cols(cx, b_glu, 4, "bglu")
        gF_t, r_gF = load_vec_cols(cx, gF, KC, "gF")
        yg, r_yg = cx.sb([128, 4, TT], BF16, "yg")
    else:
        gA_t, r_gA = load_vec_cols(cx, gA, KC, "gA")
        gM_t, r_gM = load_vec_cols(cx, gM, KC, "gM")
    for t in range(ntiles):
        tok0 = t * TT
        load_T(cx, hT_i, tok0, st.hT, st.r_h, KC)
        load_T(cx, yT_i, tok0, yin, r_yin, KC)
        load_tokmajor_T(cx, C, p_i, tok0, 2, xin, r_xin, pT, r_pT)
        if last:
            for m in range(4):
                pZ, rZ = cx.bank(4, 8)
                mm_group(S, pZ[:, :], rZ, [(wglu_t[:, k, m * 128:(m + 1) * 128], yin[:, 4 + k, :]) for k in range(4)],
                         reads=[r_wglu, r_yin])
                sl, r_sl = st.sl[st.sli % 2]
                st.sli += 1
                S.op("act", lambda e, sl=sl, pZ=pZ, m=m: e.activation(out=sl[:, :], in_=pZ[:, :], func=AF.Sigmoid,
                                                                      bias=bglu_t[:, m:m + 1]),
                     reads=[rZ, r_bglu], writes=[r_sl])
                S.op("dve", lambda e, sl=sl, m=m: e.tensor_tensor(out=yg[:, m, :], in0=yin[:, 4 + m, :], in1=sl[:, :],
                                                                  op=ALU.mult),
                     reads=[r_yin, r_sl], writes=[r_yg])
            ysrc = [yin[:, k, :] for k in range(4)] + [yg[:, k, :] for k in range(4)]
            outproj_T(cx, st, w_out, ysrc, [r_yin, r_yg])
        else:
            outproj_T(cx, st, w_out, [yin[:, k, :] for k in range(KC)], [r_yin])
        ffn_T(cx, C, st, w1, w3, w2, gB_t, r_gB)
        ple_T(cx, C, st, w_gate, wup_t, r_wup, pT, r_pT, gP_t, r_gP)
        if last:
            rmsnorm_T(cx, C, st.hT, st.r_h, KC, gF_t, r_gF, xn2, r_xn2, D, st.scratch())
            for b in range(TT // 128):
                for half in range(2):
                    bank, rb = cx.bank(4, 8)
                    fns = [(lambda e, b=b, c=c, half=half, bank=bank: e.transpose(
                        out=bank[:, c * 128:(c + 1) * 128], in_=xn2[:, half * 4 + c, b * 128:(b + 1) * 128],
                        identity=C.ident[:, :])) for c in range(4)]
                    pe_group(S, fns, reads=[r_xn2, C.r_ident], out_res=rb)
                    S.op("act", lambda e, b=b, half=half, bank=bank: e.activation(
                        out=xin[:, b, half * 512:(half + 1) * 512], in_=bank[:, :], func=AF.Copy),
                         reads=[rb], writes=[r_xin])
            S.dma("sp", lambda e, tok0=tok0: e.dma_start(
                out=out_o[tok0:tok0 + TT, :].rearrange("(b p) d -> p b d", p=128), in_=xin[:, :, :]), reads=[r_xin])
        else:
            ffn_T(cx, C, st, n1, n3, n2, gA_t, r_gA)
            rmsnorm_T(cx, C, st.hT, st.r_h, KC, gM_t, r_gM, xn2, r_xn2, D, st.scratch())
            store_T(cx, xn_o, tok0, xn2, r_xn2, KC)
            store_T(cx, hT_o, tok0, st.hT, st.r_h, KC)
    S.finish("sp")
    S.emit()
    return cx.nc


NQT = SEQ // TT
SCALE = 96 ** -0.5


def build_mla(ntiles=NQT):
    cx = Ctx()
    S = cx.S
    xnT = cx.din("xnT", [D, SEQ], BF16)
    w_cq = cx.din("w_cq", [D, 384], BF16)
    w_ckv = cx.din("w_ckv", [D, 256], BF16)
    w_kr = cx.din("w_kr", [D, 128], BF16)
    w_kr2 = cx.din("w_kr2", [D, 128], BF16)
    wq = cx.din("wq", [384, 512], BF16)
    wq2 = cx.din("wq2", [384, 512], BF16)
    wk = cx.din("wk", [256, 256], BF16)
    wv = cx.din("wv", [256, 256], BF16)
    qn = cx.din("q_norm", [384], F32)
    kvn = cx.din("kv_norm", [256], F32)
    cos_d = cx.din("cos", [32, SEQ], F32)
    sin_d = cx.din("sin", [32, SEQ], F32)
    y_o = cx.dout("yT", [256, SEQ], BF16)
    C = Consts(cx)
    wcq_t, r_wcq = load_resident(cx, w_cq, "wcq")
    wckv_t, r_wckv = load_resident(cx, w_ckv, "wckv")
    wkr_t, r_wkr = load_resident(cx, w_kr, "wkr")
    wkr2_t, r_wkr2 = load_resident(cx, w_kr2, "wkr2")
    wq_t, r_wq = load_resident(cx, wq, "wq")
    wq2_t, r_wq2 = load_resident(cx, wq2, "wq2")
    wk_t, r_wk = load_resident(cx, wk, "wk")
    wv_t, r_wv = load_resident(cx, wv, "wv")
    qn_t, r_qn = load_vec_cols(cx, qn, 3, "qn")
    kvn_t, r_kvn = load_vec_cols(cx, kvn, 2, "kvn")
    KT = [cx.sb([128, SEQ], BF16, f"KT{h}") for h in range(4)]
    Vt, r_Vt = cx.sb([128, SEQ // 128, 256], BF16, "Vtok")
    xn = [cx.sb([128, KC, TT], BF16, "xn") for _ in range(2)]
    cqT, r_cqT = cx.sb([128, 3, TT], F32, "cqT")
    cqn, r_cqn = cx.sb([128, 3, TT], BF16, "cqn")
    ckvT, r_ckvT = cx.sb([128, 2, TT], F32, "ckvT")
    ckvn, r_ckvn = cx.sb([128, 2, TT], BF16, "ckvn")
    sq, r_sq = cx.sb([128, 3, TT], BF16, "sq")
    rstd, r_rstd = cx.sb([128, TT], F32, "rstd")
    scratch = (sq, r_sq, rstd, r_rstd)
    cs, r_cs = cx.sb([128, TT], F32, "cos")
    sn, r_sn = cx.sb([128, TT], F32, "sin")
    t1, r_t1 = cx.sb([128, TT], F32, "t1")
    t2, r_t2 = cx.sb([128, TT], F32, "t2")
    QT = [cx.sb([128, TT], BF16, f"QT{h}") for h in range(4)]
    PT = [cx.sb([128, TT], BF16, "PT") for _ in range(3)]
    pti = 0
    rD, r_rD = cx.sb([128, TT], F32, "rD")
    ya, r_ya = cx.sb([128, 2, TT], BF16, "ya")
    RP = slice(64, 96)
    for t in range(ntiles):
        tok0 = t * TT
        x_t, r_x = xn[t % 2]
        load_T(cx, xnT, tok0, x_t, r_x, KC)
        S.dma("sp", lambda e, tok0=tok0: e.dma_start(out=cs[RP, :], in_=cos_d[:, tok0:tok0 + TT]), writes=[r_cs])
        S.dma("sp", lambda e, tok0=tok0: e.dma_start(out=sn[RP, :], in_=sin_d[:, tok0:tok0 + TT]), writes=[r_sn])
        for m in range(3):
            b_, rb = cx.bank(5, 8)
            mm_group(S, b_[:, :], rb, [(wcq_t[:, k, m * 128:(m + 1) * 128], x_t[:, k, :]) for k in range(KC)],
                     reads=[r_wcq, r_x])
            S.op("act", lambda e, b_=b_, m=m: e.activation(out=cqT[:, m, :], in_=b_[:, :], func=AF.Copy),
                 reads=[rb], writes=[r_cqT])
        for m in range(2):
            b_, rb = cx.bank(5, 8)
            mm_group(S, b_[:, :], rb, [(wckv_t[:, k, m * 128:(m + 1) * 128], x_t[:, k, :]) for k in range(KC)],
                     reads=[r_wckv, r_x])
            S.op("act", lambda e, b_=b_, m=m: e.activation(out=ckvT[:, m, :], in_=b_[:, :], func=AF.Copy),
                 reads=[rb], writes=[r_ckvT])
        rmsnorm_T(cx, C, cqT, r_cqT, 3, qn_t, r_qn, cqn, r_cqn, 384, scratch)
        rmsnorm_T(cx, C, ckvT, r_ckvT, 2, kvn_t, r_kvn, ckvn, r_ckvn, 256, scratch)
        b1, rb1 = cx.bank(5, 8)
        mm_group(S, b1[:, :], rb1, [(wkr_t[:, k, :], x_t[:, k, :]) for k in range(KC)], reads=[r_wkr, r_x])
        b2, rb2 = cx.bank(5, 8)
        mm_group(S, b2[:, :], rb2, [(wkr2_t[:, k, :], x_t[:, k, :]) for k in range(KC)], reads=[r_wkr2, r_x])
        S.op("dve", lambda e, b1=b1: e.tensor_tensor(out=t1[RP, :], in0=b1[RP, :], in1=cs[RP, :], op=ALU.mult),
             reads=[rb1, r_cs], writes=[r_t1])
        S.op("dve", lambda e, b2=b2: e.tensor_tensor(out=t2[RP, :], in0=b2[RP, :], in1=sn[RP, :], op=ALU.mult),
             reads=[rb2, r_sn], writes=[r_t2])
        S.op("dve", lambda e: e.tensor_tensor(out=t1[RP, :], in0=t1[RP, :], in1=t2[RP, :], op=ALU.add),
             reads=[r_t1, r_t2], writes=[r_t1])
        for h in range(4):
            kt_, r_kt = KT[h]
            S.op("act", lambda e, kt_=kt_, tok0=tok0: e.activation(out=kt_[RP, tok0:tok0 + TT], in_=t1[RP, :],
                                                                    func=AF.Copy),
                 reads=[r_t1], writes=[r_kt])
            bk, rbk = cx.bank(5, 8)
            mm_group(S, bk[0:64, :], rbk, [(wk_t[:, k, h * 64:(h + 1) * 64], ckvn[:, k, :]) for k in range(2)],
                     reads=[r_wk, r_ckvn])
            S.op("act", lambda e, kt_=kt_, bk=bk, tok0=tok0: e.activation(out=kt_[0:64, tok0:tok0 + TT],
                                                                           in_=bk[0:64, :], func=AF.Copy),
                 reads=[rbk], writes=[r_kt])
        for b in range(TT // 128):
            bv, rbv = cx.bank(5, 8)
            mm_group(S, bv[:, 0:256], rbv, [(ckvn[:, k, b * 128:(b + 1) * 128], wv_t[:, k, :]) for k in range(2)],
                     reads=[r_wv, r_ckvn])
            S.op("act", lambda e, bv=bv, b=b, t=t: e.activation(out=Vt[:, t * 4 + b, :], in_=bv[:, 0:256],
                                                                func=AF.Copy),
                 reads=[rbv], writes=[r_Vt])
        for h in range(4):
            q_, r_q = QT[h]
            bq, rbq = cx.bank(5, 8)
            mm_group(S, bq[:, :], rbq, [(wq_t[:, k, h * 128:(h + 1) * 128], cqn[:, k, :]) for k in range(3)],
                     reads=[r_wq, r_cqn])
            bq2, rbq2 = cx.bank(5, 8)
            mm_group(S, bq2[:, :], rbq2, [(wq2_t[:, k, h * 128:(h + 1) * 128], cqn[:, k, :]) for k in range(3)],
                     reads=[r_wq2, r_cqn])
            S.op("act", lambda e, q_=q_, bq=bq: e.activation(out=q_[0:64, :], in_=bq[0:64, :], func=AF.Copy),
                 reads=[rbq], writes=[r_q])
            S.op("dve", lambda e, bq=bq: e.tensor_tensor(out=t1[RP, :], in0=bq[RP, :], in1=cs[RP, :], op=ALU.mult),
                 reads=[rbq, r_cs], writes=[r_t1])
            S.op("dve", lambda e, bq2=bq2: e.tensor_tensor(out=t2[RP, :], in0=bq2[RP, :], in1=sn[RP, :], op=ALU.mult),
                 reads=[rbq2, r_sn], writes=[r_t2])
            S.op("dve", lambda e, q_=q_: e.tensor_tensor(out=q_[RP, :], in0=t1[RP, :], in1=t2[RP, :], op=ALU.add),
                 reads=[r_t1, r_t2], writes=[r_q])
        for h in range(4):
            q_, r_q = QT[h]
            kt_, r_kt = KT[h]
            po = slice(0, 64) if h % 2 == 0 else slice(64, 128)
            pO, rO = cx.psum[3]
            pD, rDn = cx.psum[4]
            nk = 4 * t + 4
            for kt in range(nk):
                pS, rS = cx.bank(0, 3)
                S.op("pe", lambda e, pS=pS, kt_=kt_, q_=q_, kt=kt: e.matmul(
                    pS[:, :], kt_[0:96, kt * 128:(kt + 1) * 128], q_[0:96, :], start=True, stop=True),
                     reads=[r_kt, r_q], writes=[rS])
                p_, r_p = PT[pti % 3]
                pti += 1
                S.op("act", lambda e, p_=p_, pS=pS: e.activation(out=p_[:, :], in_=pS[:, :], func=AF.Exp, scale=SCALE),
                     reads=[rS], writes=[r_p])
                if kt >= 4 * t:
                    j = kt - 4 * t
                    S.op("pool", lambda e, p_=p_, j=j: e.affine_select(
                        out=p_[:, :], in_=p_[:, :], pattern=[[1, TT]], compare_op=ALU.is_ge, fill=0.0,
                        base=-128 * j, channel_multiplier=-1), reads=[r_p], writes=[r_p])
                S.op("pe", lambda e, p_=p_, kt=kt, h=h, nk=nk, pO=pO, po=po: e.matmul(
                    pO[po, :], Vt[:, kt, h * 64:(h + 1) * 64], p_[:, :], start=(kt == 0), stop=(kt == nk - 1)),
                     reads=[r_Vt, r_p], writes=[rO])
                S.op("pe", lambda e, p_=p_, kt=kt, nk=nk, pD=pD, po=po: e.matmul(
                    pD[po, :], C.ones_bf[:, 0:64], p_[:, :], start=(kt == 0), stop=(kt == nk - 1)),
                     reads=[C.r_ones, r_p], writes=[rDn])
            S.op("dve", lambda e, pD=pD, po=po: e.reciprocal(out=rD[po, :], in_=pD[po, :]), reads=[rDn], writes=[r_rD])
            S.op("dve", lambda e, pO=pO, po=po, h=h: e.tensor_tensor(out=ya[po, h // 2, :], in0=pO[po, :],
                                                                     in1=rD[po, :], op=ALU.mult),
                 reads=[rO, r_rD], writes=[r_ya])
        store_T(cx, y_o, tok0, ya, r_ya, 2)
    S.finish("sp")
    S.emit()
    return cx.nc


def rope_tables_host():
    half = 16
    inv = (10000.0 ** (-np.arange(half, dtype=np.float32) / half)).astype(np.float32)
    ang = np.arange(SEQ, dtype=np.float32)[None, :] * inv[:, None]
    c = np.cos(ang).astype(np.float32)
    s = np.sin(ang).astype(np.float32)
    return np.concatenate([c, c], 0), np.concatenate([-s, s], 0)


def mla_inputs(W, hg, cast):
    w_in = W["ev_w_in"][0]
    w_cq, w_ckv, w_krope = w_in[:, 0:384], w_in[:, 384:640], w_in[:, 640:672]
    zeros = lambda a, b: np.zeros((a, b), w_in.dtype)
    sw = np.concatenate([w_krope[:, 16:], w_krope[:, :16]], 1)
    w_kr = np.concatenate([zeros(D, 64), w_krope, zeros(D, 32)], 1)
    w_kr2 = np.concatenate([zeros(D, 64), sw, zeros(D, 32)], 1)
    wqu = W["mla_w_q_up"][0]
    wq, wq2 = [], []
    for h in range(4 * hg, 4 * hg + 4):
        nope = wqu[:, h * 96:h * 96 + 64]
        r = wqu[:, h * 96 + 64:h * 96 + 96]
        rs = np.concatenate([r[:, 16:], r[:, :16]], 1)
        z64 = np.zeros((384, 64), wqu.dtype)
        z32 = np.zeros((384, 32), wqu.dtype)
        wq.append(np.concatenate([nope, r, z32], 1))
        wq2.append(np.concatenate([z64, rs, z32], 1))
    wkv = W["mla_w_kv_up"][0]
    wk = np.concatenate([wkv[:, h * 128:h * 128 + 64] for h in range(4 * hg, 4 * hg + 4)], 1)
    wv = np.concatenate([wkv[:, h * 128 + 64:h * 128 + 128] for h in range(4 * hg, 4 * hg + 4)], 1)
    cos, sin = rope_tables_host()
    c = lambda a: np.ascontiguousarray(cast(a))
    return dict(w_cq=c(w_cq), w_ckv=c(w_ckv), w_kr=c(w_kr), w_kr2=c(w_kr2), wq=c(np.concatenate(wq, 1)),
                wq2=c(np.concatenate(wq2, 1)), wk=c(wk), wv=c(wv), q_norm=W["mla_q_norm"][0],
                kv_norm=W["mla_kv_norm"][0], cos=cos, sin=sin)


def build_gla(ntiles=NQT):
    cx = Ctx(bf_bank=True)
    S = cx.S
    xnT = cx.din("xnT", [D, SEQ], BF16)
    w_gq = cx.din("w_gq", [D, 128], BF16)
    w_gk = cx.din("w_gk", [D, 128], BF16)
    w_gv = cx.din("w_gv", [D, 256], BF16)
    w_gl = cx.din("w_gl", [D, 32], BF16)
    w_gr = cx.din("w_gr", [D, 256], BF16)
    wgu = cx.din("wgu", [32, 128], BF16)
    bg = cx.din("b_gate", [128], F32)
    onrm = cx.din("out_norm", [128], F32)
    y_o = cx.dout("yT", [256, SEQ], BF16)
    C = Consts(cx)
    wgq_t, r_wgq = load_resident(cx, w_gq, "wgq")
    wgk_t, r_wgk = load_resident(cx, w_gk, "wgk")
    wgv_t, r_wgv = load_resident(cx, w_gv, "wgv")
    wgl_t, r_wgl = load_resident(cx, w_gl, "wgl")
    wgr_t, r_wgr = load_resident(cx, w_gr, "wgr")
    wgu_t, r_wgu = cx.sb([32, 128], BF16, "wgu")
    S.dma("sp", lambda e: e.dma_start(out=wgu_t[:, :], in_=wgu[:, :]), writes=[r_wgu])
    bg_t, r_bg = load_vec_cols(cx, bg, 1, "bg")
    on_t, r_on = load_vec_cols(cx, onrm, 1, "onrm")
    mask01, r_m01 = cx.sb([128, TT], F32, "mask01")
    S.op("pool", lambda e: e.memset(mask01[:, :], 1.0), writes=[r_m01])
    for c in range(TT // 64):
        S.op("pool", lambda e, c=c: e.memset(mask01[:, c * 64:c * 64 + 1], 0.0), writes=[r_m01])
    tri, r_tri = cx.sb([64, 64], F32, "tri")
    S.op("pool", lambda e: e.memset(tri[:, :], 1.0), writes=[r_tri])
    S.op("pool", lambda e: e.affine_select(out=tri[:, :], in_=tri[:, :], pattern=[[1, 64]], compare_op=ALU.is_ge,
                                           fill=0.0, base=0, channel_multiplier=-1), reads=[r_tri], writes=[r_tri])
    S32, r_S32 = cx.sb([128, 128], F32, "S32")
    Sbf, r_Sbf = cx.sb([128, 128], BF16, "Sbf")
    S.op("pool", lambda e: e.memset(S32[:, :], 0.0), writes=[r_S32])
    S.op("pool", lambda e: e.memset(Sbf[:, :], 0.0), writes=[r_Sbf])
    xn = [cx.sb([128, KC, TT], BF16, "xn") for _ in range(2)]
    glT, r_glT = cx.sb([32, TT], BF16, "glT")
    lg, r_lg = cx.sb([128, TT], F32, "lg")
    bT, r_bT = cx.sb([128, TT], F32, "bT")
    eb, r_eb = cx.sb([128, TT], F32, "eb")
    enb, r_enb = cx.sb([128, TT], F32, "enb")
    dec, r_dec = cx.sb([128, TT // 64], F32, "dec")
    qe = [cx.sb([128, TT], BF16, "qe") for _ in range(2)]
    for h in range(2):
        S.op("pool", lambda e, h=h: e.memset(qe[h][0][:, :], 0.0), writes=[qe[h][1]])
    ke32, r_ke32 = cx.sb([128, TT], F32, "ke32")
    keT, r_keT = cx.sb([128, TT], BF16, "keT")
    kend, r_kend = cx.sb([128, TT], BF16, "kend")
    identb, r_identb = cx.sb([128, 128], BF16, "identb")
    S.op("act", lambda e: e.activation(out=identb[:, :], in_=C.ident[:, :], func=AF.Copy), reads=[C.r_ident],
         writes=[r_identb])
    pTb, rTb = cx.psum[7]
    tbi = 0
    sr = [cx.sb([128, TT], F32, "sr") for _ in range(2)]
    vtok = [cx.sb([64, 256], BF16, "vtok") for _ in range(3)]
    ktok = [cx.sb([64, 128], BF16, "ktok") for _ in range(3)]
    attb = [cx.sb([64, 64], BF16, "attb") for _ in range(4)]
    sq, r_sq = cx.sb([128, TT], BF16, "sq")
    rstd, r_rstd = cx.sb([128, TT], F32, "rstd")
    on32, r_on32 = cx.sb([128, TT], F32, "on32")
    yout, r_yout = cx.sb([128, 2, TT], BF16, "yout")
    vi = ki = ai = 0
    pOg = [cx.psum[0], cx.psum[1]]
    pKV, rKV = cx.psum[2]
    for t in range(ntiles):
        tok0 = t * TT
        x_t, r_x = xn[t % 2]
        load_T(cx, xnT, tok0, x_t, r_x, KC)
        b_, rb = cx.bank(3, 7)
        mm_group(S, b_[0:32, :], rb, [(wgl_t[:, k, :], x_t[:, k, :]) for k in range(KC)], reads=[r_wgl, r_x])
        S.op("act", lambda e, b_=b_: e.activation(out=glT[:, :], in_=b_[0:32, :], func=AF.Copy),
             reads=[rb], writes=[r_glT])
        bz, rbz = cx.bank(3, 7)
        S.op("pe", lambda e, bz=bz: e.matmul(bz[:, :], wgu_t[:, :], glT[:, :], start=True, stop=True),
             reads=[r_wgu, r_glT], writes=[rbz])
        S.op("act", lambda e, bz=bz: e.activation(out=lg[:, :], in_=bz[:, :], func=AF.Sigmoid, bias=bg_t[:, 0:1]),
             reads=[rbz, r_bg], writes=[r_lg])
        S.op("act", lambda e: e.activation(out=lg[:, :], in_=lg[:, :], func=AF.Ln), reads=[r_lg], writes=[r_lg])
        S.op("dve", lambda e: e.tensor_tensor_scan(out=bT[:, :], data0=mask01[:, :], data1=lg[:, :], initial=0.0,
                                                   op0=ALU.mult, op1=ALU.add),
             reads=[r_m01, r_lg], writes=[r_bT])
        S.op("act", lambda e: e.activation(out=eb[:, :], in_=bT[:, :], func=AF.Exp, scale=1.0 / 16),
             reads=[r_bT], writes=[r_eb])
        S.op("act", lambda e: e.activation(out=enb[:, :], in_=bT[:, :], func=AF.Exp, scale=-1.0 / 16),
             reads=[r_bT], writes=[r_enb])
        S.op("act", lambda e: e.activation(out=dec[:, :], in_=bT[:, 63::64], func=AF.Exp, scale=1.0 / 16),
             reads=[r_bT], writes=[r_dec])
        bq, rbq = cx.bank(3, 7)
        mm_group(S, bq[:, :], rbq, [(wgq_t[:, k, :], x_t[:, k, :]) for k in range(KC)], reads=[r_wgq, r_x])
        for h in range(2):
            hs = slice(h * 64, (h + 1) * 64)
            S.op("dve", lambda e, bq=bq, h=h, hs=hs: e.scalar_tensor_tensor(out=qe[h][0][hs, :], in0=bq[hs, :],
                                                                            scalar=0.125, in1=eb[hs, :],
                                                                            op0=ALU.mult, op1=ALU.mult),
                 reads=[rbq, r_eb], writes=[qe[h][1]])
        bk, rbk = cx.bank(3, 7)
        mm_group(S, bk[:, :], rbk, [(wgk_t[:, k, :], x_t[:, k, :]) for k in range(KC)], reads=[r_wgk, r_x])
        S.op("dve", lambda e, bk=bk: e.tensor_tensor(out=ke32[:, :], in0=bk[:, :], in1=enb[:, :], op=ALU.mult),
             reads=[rbk, r_enb], writes=[r_ke32])
        S.op("act", lambda e: e.activation(out=keT[:, :], in_=ke32[:, :], func=AF.Copy), reads=[r_ke32], writes=[r_keT])
        for c in range(TT // 64):
            S.op("dve", lambda e, c=c: e.tensor_scalar(out=kend[:, c * 64:(c + 1) * 64], in0=ke32[:, c * 64:(c + 1) * 64],
                                                       scalar1=dec[:, c:c + 1], scalar2=None, op0=ALU.mult),
                 reads=[r_ke32, r_dec], writes=[r_kend])
        for h in range(2):
            br, rbr = cx.bank(3, 7)
            mm_group(S, br[:, :], rbr, [(wgr_t[:, k, h * 128:(h + 1) * 128], x_t[:, k, :]) for k in range(KC)],
                     reads=[r_wgr, r_x])
            S.op("act", lambda e, br=br, h=h: e.activation(out=sr[h][0][:, :], in_=br[:, :], func=AF.Silu),
                 reads=[rbr], writes=[sr[h][1]])
        for c in range(TT // 64):
            cs_ = slice(c * 64, (c + 1) * 64)
            bv, rbv = cx.bank(3, 7)
            mm_group(S, bv[0:64, 0:256], rbv, [(x_t[:, k, cs_], wgv_t[:, k, :]) for k in range(KC)], reads=[r_wgv, r_x])
            v_, r_v = vtok[vi % 3]
            vi += 1
            S.op("act", lambda e, bv=bv, v_=v_: e.activation(out=v_[:, :], in_=bv[0:64, 0:256], func=AF.Copy),
                 reads=[rbv], writes=[r_v])
            tb0 = (tbi % 8) * 128
            tbi += 1
            S.op("pe", lambda e, cs_=cs_, tb0=tb0: e.transpose(out=pTb[0:64, tb0:tb0 + 128], in_=kend[:, cs_],
                                                                identity=identb[:, :]),
                 reads=[r_kend, r_identb], writes=[rTb])
            k_, r_k = ktok[ki % 3]
            ki += 1
            S.op("act", lambda e, k_=k_, tb0=tb0: e.activation(out=k_[:, :], in_=pTb[0:64, tb0:tb0 + 128], func=AF.Copy),
                 reads=[rTb], writes=[r_k])
            for h in range(2):
                hs = slice(h * 64, (h + 1) * 64)
                q_, r_q = qe[h]
                ba, rba = cx.bank(3, 7)
                S.op("pe", lambda e, ba=ba, q_=q_, cs_=cs_: e.matmul(ba[0:64, 0:64], keT[:, cs_], q_[:, cs_],
                                                                     start=True, stop=True),
                     reads=[r_keT, r_q], writes=[rba])
                a_, r_a = attb[ai % 4]
                ai += 1
                S.op("dve", lambda e, ba=ba, a_=a_: e.tensor_tensor(out=a_[:, :], in0=ba[0:64, 0:64], in1=tri[:, :],
                                                                    op=ALU.mult),
                     reads=[rba, r_tri], writes=[r_a])
                po, rpo = pOg[h]
                pe_group(S, [
                    (lambda e, po=po, v_=v_, a_=a_, h=h, cs_=cs_: e.matmul(po[:, cs_], v_[:, h * 128:(h + 1) * 128],
                                                                           a_[:, :], start=True, stop=False)),
                    (lambda e, po=po, q_=q_, cs_=cs_: e.matmul(po[:, cs_], Sbf[:, :], q_[:, cs_],
                                                               start=False, stop=True)),
                ], reads=[r_v, r_a, r_Sbf, r_q], out_res=rpo)
                S.op("pe", lambda e, k_=k_, v_=v_, hs=hs, h=h: e.matmul(pKV[hs, 0:128], k_[:, hs],
                                                                        v_[:, h * 128:(h + 1) * 128],
                                                                        start=True, stop=True),
                     reads=[r_k, r_v], writes=[rKV])
            S.op("dve", lambda e, c=c: e.scalar_tensor_tensor(out=S32[:, :], in0=S32[:, :], scalar=dec[:, c:c + 1],
                                                              in1=pKV[:, 0:128], op0=ALU.mult, op1=ALU.add),
                 reads=[r_S32, r_dec, rKV], writes=[r_S32])
            S.op("act", lambda e: e.activation(out=Sbf[:, :], in_=S32[:, :], func=AF.Copy),
                 reads=[r_S32], writes=[r_Sbf])
        for h in range(2):
            po, rpo = pOg[h]
            S.op("act", lambda e, po=po: e.activation(out=sq[:, :], in_=po[:, :], func=AF.Square),
                 reads=[rpo], writes=[r_sq])
            bs, rbs = cx.bank(3, 7)
            S.op("pe", lambda e, bs=bs: e.matmul(bs[:, :], C.ones_bf[:, :], sq[:, :], start=True, stop=True),
                 reads=[C.r_ones, r_sq], writes=[rbs])
            S.op("act", lambda e, bs=bs: e.activation(out=rstd[:, :], in_=bs[:, :], func=AF.Sqrt, scale=1.0 / 128,
                                                      bias=EPS), reads=[rbs], writes=[r_rstd])
            S.op("dve", lambda e: e.reciprocal(out=rstd[:, :], in_=rstd[:, :]), reads=[r_rstd], writes=[r_rstd])
            S.op("dve", lambda e, po=po: e.scalar_tensor_tensor(out=on32[:, :], in0=po[:, :], scalar=on_t[:, 0:1],
                                                                in1=rstd[:, :], op0=ALU.mult, op1=ALU.mult),
                 reads=[rpo, r_on, r_rstd], writes=[r_on32])
            S.op("dve", lambda e, h=h: e.tensor_tensor(out=yout[:, h, :], in0=on32[:, :], in1=sr[h][0][:, :],
                                                       op=ALU.mult),
                 reads=[r_on32, sr[h][1]], writes=[r_yout])
        store_T(cx, y_o, tok0, yout, r_yout, 2)
    S.finish("sp")
    S.emit()
    return cx.nc


def gla_inputs(W, hg, cast):
    w_in = W["ev_w_in"][0]
    o = 672
    gq, gk, gv = w_in[:, o:o + 256], w_in[:, o + 256:o + 512], w_in[:, o + 512:o + 1024]
    gl, gr = w_in[:, o + 1024:o + 1040], w_in[:, o + 1040:o + 1552]
    hs = slice(128 * hg, 128 * hg + 128)
    vs = slice(256 * hg, 256 * hg + 256)
    c = lambda a: np.ascontiguousarray(cast(a))
    gl = np.concatenate([gl, np.zeros_like(gl)], 1)
    gu = W["gla_w_gate_up"][0][:, hs]
    gu = np.concatenate([gu, np.zeros_like(gu)], 0)
    return dict(w_gq=c(gq[:, hs]), w_gk=c(gk[:, hs]), w_gv=c(gv[:, vs]), w_gl=c(gl), w_gr=c(gr[:, vs]),
                wgu=c(gu), b_gate=np.ascontiguousarray(W["gla_b_gate"][0][hs]),
                out_norm=W["gla_out_norm"][0])


def gelu_tanh_mul(cx, src_ap, r_src, other_ap, r_other, out_ap, r_out, tmp, P=slice(0, 128)):
    S = cx.S
    t1, r1, t2, r2 = tmp
    S.op("act", lambda e: e.activation(out=t1[P, :], in_=src_ap, func=AF.Square), reads=[r_src], writes=[r1])
    S.op("dve", lambda e: e.tensor_scalar(out=t1[P, :], in0=t1[P, :], scalar1=0.044715, scalar2=1.0,
                                          op0=ALU.mult, op1=ALU.add), reads=[r1], writes=[r1])
    S.op("dve", lambda e: e.tensor_tensor(out=t1[P, :], in0=src_ap, in1=t1[P, :], op=ALU.mult),
         reads=[r_src, r1], writes=[r1])
    S.op("act", lambda e: e.activation(out=t1[P, :], in_=t1[P, :], func=AF.Sigmoid, scale=1.5957691216),
         reads=[r1], writes=[r1])
    if other_ap is None:
        S.op("dve", lambda e: e.tensor_tensor(out=out_ap, in0=src_ap, in1=t1[P, :], op=ALU.mult),
             reads=[r_src, r1], writes=[r_out])
    else:
        S.op("dve", lambda e: e.tensor_tensor(out=t2[P, :], in0=src_ap, in1=t1[P, :], op=ALU.mult),
             reads=[r_src, r1], writes=[r2])
        S.op("dve", lambda e: e.tensor_tensor(out=out_ap, in0=t2[P, :], in1=other_ap, op=ALU.mult),
             reads=[r2, r_other], writes=[r_out])


def build_rglru(ntiles=NQT):
    cx = Ctx()
    S = cx.S
    xnT = cx.din("xnT", [D, SEQ], BF16)
    w_g = cx.din("w_g", [D, 256], BF16)
    w_r = cx.din("w_r", [D, 256], BF16)
    wa = cx.din("wa_bd", [256, 128], BF16)
    wi = cx.din("wi_bd", [256, 128], BF16)
    cw = cx.din("conv_w", [4, 256], F32)
    cb = cx.din("conv_b", [256], F32)
    ba_ = cx.din("b_a", [256], F32)
    bi_ = cx.din("b_i", [256], F32)
    lam = cx.din("lam", [256], F32)
    y_o = cx.dout("yT", [256, SEQ], BF16)
    C = Consts(cx)
    wg_t, r_wg = load_resident(cx, w_g, "wg")
    wr_t, r_wr = load_resident(cx, w_r, "wr")
    wa_t, r_wa = load_resident(cx, wa, "wa")
    wi_t, r_wi = load_resident(cx, wi, "wi")
    cwk = [load_vec_cols(cx, cw[k, :], 2, f"cw{k}") for k in range(4)]
    cb_t, r_cb = load_vec_cols(cx, cb, 2, "cb")
    ba_t, r_ba = load_vec_cols(cx, ba_, 2, "ba")
    bi_t, r_bi = load_vec_cols(cx, bi_, 2, "bi")
    lam_t, r_lam = load_vec_cols(cx, lam, 2, "lam")
    cA, r_cA = cx.sb([128, 2], F32, "cA")
    S.op("act", lambda e: e.activation(out=cA[:, :], in_=lam_t[:, :], func=AF.Exp, scale=-1.0), reads=[r_lam], writes=[r_cA])
    S.op("act", lambda e: e.activation(out=cA[:, :], in_=cA[:, :], func=AF.Ln, bias=1.0), reads=[r_cA], writes=[r_cA])
    S.op("dve", lambda e: e.tensor_scalar(out=cA[:, :], in0=cA[:, :], scalar1=-8.0, scalar2=None, op0=ALU.mult),
         reads=[r_cA], writes=[r_cA])
    xn = [cx.sb([128, KC, TT], BF16, "xn") for _ in range(2)]
    xr = [cx.sb([128, 3 + TT], F32, "xr") for _ in range(2)]
    for ct in range(2):
        S.op("pool", lambda e, ct=ct: e.memset(xr[ct][0][:, 0:3], 0.0), writes=[xr[ct][1]])
    carry, r_carry = cx.sb([128, 4], F32, "carry")
    S.op("pool", lambda e: e.memset(carry[:, :], 0.0), writes=[r_carry])
    xc, r_xc = cx.sb([128, TT], F32, "xc")
    xcb, r_xcb = cx.sb([128, TT], BF16, "xcb")
    rg, r_rg = cx.sb([128, TT], F32, "rg")
    ig, r_ig = cx.sb([128, TT], F32, "ig")
    av, r_av = cx.sb([128, TT], F32, "av")
    om, r_om = cx.sb([128, TT], F32, "om")
    bx, r_bx = cx.sb([128, TT], F32, "bx")
    hh, r_hh = cx.sb([128, TT], F32, "hh")
    t1, r_t1 = cx.sb([128, TT], F32, "t1")
    t2, r_t2 = cx.sb([128, TT], F32, "t2")
    yout, r_yout = cx.sb([128, 2, TT], BF16, "yout")
    for t in range(ntiles):
        tok0 = t * TT
        x_t, r_x = xn[t % 2]
        load_T(cx, xnT, tok0, x_t, r_x, KC)
        for ct in range(2):
            xr_, r_xr = xr[ct]
            cs_ = slice(ct * 128, (ct + 1) * 128)
            bR, rbR = cx.bank(0, 8)
            mm_group(S, bR[:, :], rbR, [(wr_t[:, k, cs_], x_t[:, k, :]) for k in range(KC)], reads=[r_wr, r_x])
            S.op("act", lambda e, xr_=xr_, bR=bR: e.activation(out=xr_[:, 3:3 + TT], in_=bR[:, :], func=AF.Copy),
                 reads=[rbR], writes=[r_xr])
            S.op("act", lambda e, xr_=xr_, ct=ct: e.activation(out=xc[:, :], in_=xr_[:, 3:3 + TT], func=AF.Identity,
                                                               scale=cwk[3][0][:, ct:ct + 1], bias=cb_t[:, ct:ct + 1]),
                 reads=[r_xr, cwk[3][1], r_cb], writes=[r_xc])
            for k in range(3):
                S.op("dve", lambda e, xr_=xr_, ct=ct, k=k: e.scalar_tensor_tensor(
                    out=xc[:, :], in0=xr_[:, k:k + TT], scalar=cwk[k][0][:, ct:ct + 1], in1=xc[:, :],
                    op0=ALU.mult, op1=ALU.add), reads=[r_xr, cwk[k][1], r_xc], writes=[r_xc])
            S.op("act", lambda e: e.activation(out=xcb[:, :], in_=xc[:, :], func=AF.Copy), reads=[r_xc], writes=[r_xcb])
            S.op("act", lambda e, xr_=xr_: e.activation(out=xr_[:, 0:3], in_=xr_[:, TT:TT + 3], func=AF.Copy),
                 reads=[r_xr, r_xc], writes=[r_xr])
            bA, rbA = cx.bank(0, 8)
            S.op("pe", lambda e, bA=bA, ct=ct: e.matmul(bA[:, :], wa_t[:, ct, :], xcb[:, :], start=True, stop=True),
                 reads=[r_wa, r_xcb], writes=[rbA])
            bI, rbI = cx.bank(0, 8)
            S.op("pe", lambda e, bI=bI, ct=ct: e.matmul(bI[:, :], wi_t[:, ct, :], xcb[:, :], start=True, stop=True),
                 reads=[r_wi, r_xcb], writes=[rbI])
            S.op("act", lambda e, bA=bA, ct=ct: e.activation(out=rg[:, :], in_=bA[:, :], func=AF.Sigmoid,
                                                             bias=ba_t[:, ct:ct + 1]), reads=[rbA, r_ba], writes=[r_rg])
            S.op("act", lambda e, bI=bI, ct=ct: e.activation(out=ig[:, :], in_=bI[:, :], func=AF.Sigmoid,
                                                             bias=bi_t[:, ct:ct + 1]), reads=[rbI, r_bi], writes=[r_ig])
            S.op("act", lambda e, ct=ct: e.activation(out=av[:, :], in_=rg[:, :], func=AF.Exp, scale=cA[:, ct:ct + 1]),
                 reads=[r_rg, r_cA], writes=[r_av])
            S.op("act", lambda e: e.activation(out=om[:, :], in_=av[:, :], func=AF.Square), reads=[r_av], writes=[r_om])
            S.op("act", lambda e: e.activation(out=om[:, :], in_=om[:, :], func=AF.Sqrt, scale=-1.0, bias=1.0),
                 reads=[r_om], writes=[r_om])
            S.op("dve", lambda e: e.tensor_tensor(out=bx[:, :], in0=ig[:, :], in1=xc[:, :], op=ALU.mult),
                 reads=[r_ig, r_xc], writes=[r_bx])
            S.op("dve", lambda e: e.tensor_tensor(out=bx[:, :], in0=bx[:, :], in1=om[:, :], op=ALU.mult),
                 reads=[r_bx, r_om], writes=[r_bx])
            S.op("act", lambda e, ct=ct: e.activation(out=carry[:, 2:3], in_=carry[:, ct:ct + 1], func=AF.Identity,
                                                      scale=av[:, 0:1], bias=bx[:, 0:1]),
                 reads=[r_carry, r_av, r_bx], writes=[r_carry])
            S.op("act", lambda e: e.activation(out=bx[:, 0:1], in_=carry[:, 2:3], func=AF.Copy),
                 reads=[r_carry], writes=[r_bx])
            S.op("dve", lambda e: e.tensor_tensor_scan(out=hh[:, :], data0=av[:, :], data1=bx[:, :],
                                                       initial=0.0, op0=ALU.mult, op1=ALU.add),
                 reads=[r_av, r_bx], writes=[r_hh])
            S.op("act", lambda e, ct=ct: e.activation(out=carry[:, ct:ct + 1], in_=hh[:, TT - 1:TT], func=AF.Copy),
                 reads=[r_hh], writes=[r_carry])
            bG, rbG = cx.bank(0, 8)
            mm_group(S, bG[:, :], rbG, [(wg_t[:, k, cs_], x_t[:, k, :]) for k in range(KC)], reads=[r_wg, r_x])
            gelu_tanh_mul(cx, bG[:, :], rbG, hh[:, :], r_hh, yout[:, ct, :], r_yout, (t1, r_t1, t2, r_t2))
        store_T(cx, y_o, tok0, yout, r_yout, 2)
    S.finish("sp")
    S.emit()
    return cx.nc


def block_diag_pairs(w):
    nb = w.shape[0]
    out = np.zeros((nb // 2, 128, 128), w.dtype)
    for i in range(nb):
        j, o = divmod(i, 2)
        out[j, o * 64:(o + 1) * 64, o * 64:(o + 1) * 64] = w[i]
    return out.reshape(nb // 2 * 128, 128)


def rglru_inputs(W, half, cast):
    w_in = W["od_w_in"][0]
    cs_ = slice(256 * half, 256 * half + 256)
    bs_ = slice(4 * half, 4 * half + 4)
    c = lambda a: np.ascontiguousarray(cast(a))
    return dict(w_g=c(w_in[:, 0:512][:, cs_]), w_r=c(w_in[:, 512:1024][:, cs_]),
                wa_bd=c(block_diag_pairs(W["rg_w_a"][0][bs_])), wi_bd=c(block_diag_pairs(W["rg_w_i"][0][bs_])),
                conv_w=np.ascontiguousarray(W["rg_conv_w"][0][:, cs_]), conv_b=np.ascontiguousarray(W["rg_conv_b"][0][cs_]),
                b_a=np.ascontiguousarray(W["rg_b_a"][0][cs_]), b_i=np.ascontiguousarray(W["rg_b_i"][0][cs_]),
                lam=np.ascontiguousarray(W["rg_lambda"][0][cs_]))


NST = 8
HALF_PI = 1.5707963267948966


def build_s5(ntiles=NQT):
    cx = Ctx()
    S = cx.S
    xnT = cx.din("xnT", [D, SEQ], BF16)
    w_u = cx.din("w_u", [D, 256], BF16)
    lr_d = cx.din("lr", [128, NST], F32)
    li_d = cx.din("li", [128, NST], F32)
    ldt_d = cx.din("ldt", [128, NST], F32)
    bre_d = cx.din("b_re", [128, NST * 16], F32)
    bim_d = cx.din("b_im", [128, NST * 16], F32)
    lcre_d = cx.din("lc_re", [NST * 128, 128], F32)
    lcim_d = cx.din("lc_im", [NST * 128, 128], F32)
    d_d = cx.din("d", [256], F32)
    y_o = cx.dout("yT", [256, SEQ], BF16)
    C = Consts(cx)
    wu_t, r_wu = load_resident(cx, w_u, "wu")
    d_t, r_d = load_vec_cols(cx, d_d, 2, "d")

    def small(name, src=None, n=NST):
        t, r = cx.sb([128, n], F32, name)
        if src is not None:
            S.dma("sp", lambda e: e.dma_start(out=t[:, :], in_=src[:, :]), writes=[r])
        return t, r

    lr, r_lr = small("lr", lr_d)
    li, r_li = small("li", li_d)
    dt, r_dt = small("dt", ldt_d)
    bre, r_bre = small("bre", bre_d, NST * 16)
    bim, r_bim = small("bim", bim_d, NST * 16)

    def act(out, in_, func, reads, writes, **kw):
        S.op("act", lambda e: e.activation(out=out, in_=in_, func=func, **kw), reads=reads, writes=writes)

    def tt(out, a, b, op, reads, writes, eng="dve"):
        S.op(eng, lambda e: e.tensor_tensor(out=out, in0=a, in1=b, op=op), reads=reads, writes=writes)

    def ts(out, a, s1, op0, reads, writes, s2=None, op1=None, eng="dve"):
        if op1 is None:
            S.op(eng, lambda e: e.tensor_scalar(out=out, in0=a, scalar1=s1, scalar2=None, op0=op0), reads=reads,
                 writes=writes)
        else:
            S.op(eng, lambda e: e.tensor_scalar(out=out, in0=a, scalar1=s1, scalar2=s2, op0=op0, op1=op1),
                 reads=reads, writes=writes)

    def stt(out, a, s, b, op0, op1, reads, writes):
        S.op("dve", lambda e: e.scalar_tensor_tensor(out=out, in0=a, scalar=s, in1=b, op0=op0, op1=op1),
             reads=reads, writes=writes)

    act(dt[:, :], dt[:, :], AF.Exp, [r_dt], [r_dt])
    lrd, r_lrd = small("lrd")
    th, r_th = small("th")
    tt(lrd[:, :], lr[:, :], dt[:, :], ALU.mult, [r_lr, r_dt], [r_lrd])
    tt(th[:, :], li[:, :], dt[:, :], ALU.mult, [r_li, r_dt], [r_th])
    mag, r_mag = small("mag")
    act(mag[:, :], lrd[:, :], AF.Exp, [r_lrd], [r_mag])
    cc, r_cc = small("cc")
    ss, r_ss = small("ss")
    ta, r_ta = small("ta")
    tb, r_tb = small("tb")
    hp, r_hp = small("hp", n=1)
    S.op("pool", lambda e: e.memset(hp[:, :], HALF_PI), writes=[r_hp])
    act(ss[:, :], th[:, :], AF.Sin, [r_th], [r_ss], scale=1.0 / 64)
    act(cc[:, :], th[:, :], AF.Sin, [r_th, r_hp], [r_cc], scale=1.0 / 64, bias=hp[:, 0:1])

    def cdouble(c_, r_c, s_, r_s):
        tt(ta[:, :], c_, c_, ALU.mult, [r_c], [r_ta])
        tt(tb[:, :], s_, s_, ALU.mult, [r_s], [r_tb])
        stt(s_, c_, 2.0, s_, ALU.mult, ALU.mult, [r_c, r_s], [r_s])
        tt(c_, ta[:, :], tb[:, :], ALU.subtract, [r_ta, r_tb], [r_c])

    for _ in range(6):
        cdouble(cc[:, :], r_cc, ss[:, :], r_ss)
    NP = 10
    pw_r, r_pwr = small("pwr", n=NP * NST)
    pw_i, r_pwi = small("pwi", n=NP * NST)
    pw_ni, r_pwni = small("pwni", n=NP * NST)
    act(pw_r[:, 0:NST], cc[:, :], AF.Copy, [r_cc], [r_pwr])
    act(pw_i[:, 0:NST], ss[:, :], AF.Copy, [r_ss], [r_pwi])
    for k in range(1, NP):
        a0, a1, b0, b1 = (k - 1) * NST, k * NST, k * NST, (k + 1) * NST
        tt(ta[:, :], pw_r[:, a0:a1], pw_r[:, a0:a1], ALU.mult, [r_pwr], [r_ta])
        tt(tb[:, :], pw_i[:, a0:a1], pw_i[:, a0:a1], ALU.mult, [r_pwi], [r_tb])
        stt(pw_i[:, b0:b1], pw_r[:, a0:a1], 2.0, pw_i[:, a0:a1], ALU.mult, ALU.mult, [r_pwr, r_pwi], [r_pwi])
        tt(pw_r[:, b0:b1], ta[:, :], tb[:, :], ALU.subtract, [r_ta, r_tb], [r_pwr])
    ts(pw_ni[:, :], pw_i[:, :], -1.0, ALU.mult, [r_pwi], [r_pwni])
    ar, r_ar = small("ar")
    ai, r_ai = small("ai")
    tt(ar[:, :], mag[:, :], cc[:, :], ALU.mult, [r_mag, r_cc], [r_ar])
    tt(ai[:, :], mag[:, :], ss[:, :], ALU.mult, [r_mag, r_ss], [r_ai])
    ts(ar[:, :], ar[:, :], -1.0, ALU.add, [r_ar], [r_ar])
    den, r_den = small("den")
    tt(ta[:, :], lr[:, :], lr[:, :], ALU.mult, [r_lr], [r_ta])
    tt(tb[:, :], li[:, :], li[:, :], ALU.mult, [r_li], [r_tb])
    tt(den[:, :], ta[:, :], tb[:, :], ALU.add, [r_ta, r_tb], [r_den])
    S.op("dve", lambda e: e.reciprocal(out=den[:, :], in_=den[:, :]), reads=[r_den], writes=[r_den])
    cre, r_cre = small("cre")
    cim, r_cim = small("cim")
    ncim, r_ncim = small("ncim")
    tt(ta[:, :], ar[:, :], lr[:, :], ALU.mult, [r_ar, r_lr], [r_ta])
    tt(tb[:, :], ai[:, :], li[:, :], ALU.mult, [r_ai, r_li], [r_tb])
    tt(cre[:, :], ta[:, :], tb[:, :], ALU.add, [r_ta, r_tb], [r_cre])
    tt(cre[:, :], cre[:, :], den[:, :], ALU.mult, [r_cre, r_den], [r_cre])
    tt(ta[:, :], ai[:, :], lr[:, :], ALU.mult, [r_ai, r_lr], [r_ta])
    tt(tb[:, :], ar[:, :], li[:, :], ALU.mult, [r_ar, r_li], [r_tb])
    tt(cim[:, :], ta[:, :], tb[:, :], ALU.subtract, [r_ta, r_tb], [r_cim])
    tt(cim[:, :], cim[:, :], den[:, :], ALU.mult, [r_cim, r_den], [r_cim])
    ts(ncim[:, :], cim[:, :], -1.0, ALU.mult, [r_cim], [r_ncim])
    LB = []
    LC = []
    BBr, r_BBr = cx.sb([128, 128], F32, "BBr")
    BBi, r_BBi = cx.sb([128, 128], F32, "BBi")
    lc32, r_lc32 = cx.sb([128, 128], F32, "lc32")
    for st in range(NST):
        S.op("pool", lambda e: e.memset(BBr[:, :], 0.0), writes=[r_BBr])
        S.op("pool", lambda e: e.memset(BBi[:, :], 0.0), writes=[r_BBi])
        for j in range(2):
            gl = (2 * st + j) % 8
            P_ = slice(j * 64, (j + 1) * 64)
            cols = slice(gl * 16, (gl + 1) * 16)
            br_ = bre[P_, st * 16:(st + 1) * 16]
            bi_ = bim[P_, st * 16:(st + 1) * 16]
            ts(BBr[P_, cols], br_, cre[P_, st:st + 1], ALU.mult, [r_bre, r_cre], [r_BBr])
            stt(BBr[P_, cols], bi_, ncim[P_, st:st + 1], BBr[P_, cols], ALU.mult, ALU.add, [r_bim, r_ncim, r_BBr], [r_BBr])
            ts(BBi[P_, cols], bi_, cre[P_, st:st + 1], ALU.mult, [r_bim, r_cre], [r_BBi])
            stt(BBi[P_, cols], br_, cim[P_, st:st + 1], BBi[P_, cols], ALU.mult, ALU.add, [r_bre, r_cim, r_BBi], [r_BBi])
        pair = []
        for src, r_src in ((BBr, r_BBr), (BBi, r_BBi)):
            bank, rb = cx.bank(0, 8)
            S.op("pe", lambda e, bank=bank, src=src: e.transpose(out=bank[:, 0:128], in_=src[:, :], identity=C.ident[:, :]),
                 reads=[r_src, C.r_ident], writes=[rb])
            lt, r_lt = cx.sb([128, 128], BF16, "LB")
            act(lt[:, :], bank[:, 0:128], AF.Copy, [rb], [r_lt])
            pair.append((lt, r_lt))
        LB.append(pair)
        pair = []
        for src_d, sc in ((lcre_d, 1.0), (lcim_d, -1.0)):
            S.dma("sp", lambda e, src_d=src_d, st=st: e.dma_start(out=lc32[:, :], in_=src_d[st * 128:(st + 1) * 128, :]),
                  writes=[r_lc32])
            lt, r_lt = cx.sb([128, 128], BF16, "LC")
            act(lt[:, :], lc32[:, :], AF.Copy, [r_lc32], [r_lt], scale=sc)
            pair.append((lt, r_lt))
        LC.append(pair)
    ones32, r_ones32 = cx.sb([128, TT], F32, "ones32")
    S.op("pool", lambda e: e.memset(ones32[:, :], 1.0), writes=[r_ones32])
    cosT, sinT, rtab = [], [], []
    for st in range(NST):
        ct_, r_ct = cx.sb([128, TT], F32, "cosT")
        st_, r_st = cx.sb([128, TT], F32, "sinT")
        rt_, r_rt = cx.sb([128, TT], F32, "rtab")
        S.op("pool", lambda e, ct_=ct_: e.memset(ct_[:, 0:1], 1.0), writes=[r_ct])
        S.op("pool", lambda e, st_=st_: e.memset(st_[:, 0:1], 0.0), writes=[r_st])
        for k in range(9):
            n = 1 << k
            pr = pw_r[:, k * NST + st:k * NST + st + 1]
            pi = pw_i[:, k * NST + st:k * NST + st + 1]
            npi = pw_ni[:, k * NST + st:k * NST + st + 1]
            ts(ct_[:, n:2 * n], ct_[:, 0:n], pr, ALU.mult, [r_ct, r_pwr], [r_ct])
            stt(ct_[:, n:2 * n], st_[:, 0:n], npi, ct_[:, n:2 * n], ALU.mult, ALU.add, [r_st, r_pwni, r_ct], [r_ct])
            ts(st_[:, n:2 * n], ct_[:, 0:n], pi, ALU.mult, [r_ct, r_pwi], [r_st])
            stt(st_[:, n:2 * n], st_[:, 0:n], pr, st_[:, n:2 * n], ALU.mult, ALU.add, [r_st, r_pwr], [r_st])
        ts(rt_[:, :], ones32[:, :], mag[:, st:st + 1], ALU.mult, [r_ones32, r_mag], [r_rt])
        cosT.append((ct_, r_ct))
        sinT.append((st_, r_st))
        rtab.append((rt_, r_rt))
    PTr = pw_r[:, 9 * NST:10 * NST]
    PTi = pw_i[:, 9 * NST:10 * NST]
    car_r, r_car_r = small("car_r")
    car_i, r_car_i = small("car_i")
    S.op("pool", lambda e: e.memset(car_r[:, :], 0.0), writes=[r_car_r])
    S.op("pool", lambda e: e.memset(car_i[:, :], 0.0), writes=[r_car_i])
    cz, r_cz = small("cz", n=4)
    ini, r_ini = small("ini", n=2)
    xn = [cx.sb([128, KC, TT], BF16, "xn") for _ in range(2)]
    u32s = [cx.sb([128, 2, TT], F32, "u32") for _ in range(2)]
    ubs = [cx.sb([128, 2, TT], BF16, "ub") for _ in range(2)]
    ypre, r_ypre = cx.sb([128, TT], F32, "ypre")
    m = [cx.sb([128, TT], F32, f"m{i}") for i in range(4)]
    wre, r_wre = cx.sb([128, TT], F32, "wre")
    wim, r_wim = cx.sb([128, TT], F32, "wim")
    zre, r_zre = cx.sb([128, TT], F32, "zre")
    zim, r_zim = cx.sb([128, TT], F32, "zim")
    xrb, r_xrb = cx.sb([128, NST, TT], BF16, "xrb")
    xib, r_xib = cx.sb([128, NST, TT], BF16, "xib")
    yss, r_yss = cx.sb([128, TT], F32, "yss")
    t1, r_t1 = cx.sb([128, TT], F32, "t1")
    t2, r_t2 = cx.sb([128, TT], F32, "t2")
    yout, r_yout = cx.sb([128, 2, TT], BF16, "yout")
    for t in range(ntiles):
        tok0 = t * TT
        x_t, r_x = xn[t % 2]
        u32, r_u32 = u32s[t % 2]
        ub, r_ub = ubs[t % 2]
        load_T(cx, xnT, tok0, x_t, r_x, KC)
        for ct in range(2):
            bU, rbU = cx.bank(0, 8)
            mm_group(S, bU[:, :], rbU, [(wu_t[:, k, ct * 128:(ct + 1) * 128], x_t[:, k, :]) for k in range(KC)],
                     reads=[r_wu, r_x])
            act(u32[:, ct, :], bU[:, :], AF.Copy, [rbU], [r_u32])
            act(ub[:, ct, :], bU[:, :], AF.Copy, [rbU], [r_ub])
        for st in range(NST):
            ct = st // 4
            ct_, r_ct = cosT[st]
            st_, r_st = sinT[st]
            rt_, r_rt = rtab[st]
            pBr, rBr = cx.bank(0, 8)
            S.op("pe", lambda e, pBr=pBr, st=st, ct=ct, ub=ub: e.matmul(pBr[:, :], LB[st][0][0][:, :], ub[:, ct, :],
                                                                 start=True, stop=True),
                 reads=[LB[st][0][1], r_ub], writes=[rBr])
            pBi, rBi = cx.bank(0, 8)
            S.op("pe", lambda e, pBi=pBi, st=st, ct=ct, ub=ub: e.matmul(pBi[:, :], LB[st][1][0][:, :], ub[:, ct, :],
                                                                 start=True, stop=True),
                 reads=[LB[st][1][1], r_ub], writes=[rBi])
            tt(m[0][0][:, :], pBr[:, :], ct_[:, :], ALU.mult, [rBr, r_ct], [m[0][1]])
            tt(m[1][0][:, :], pBi[:, :], st_[:, :], ALU.mult, [rBi, r_st], [m[1][1]])
            tt(m[2][0][:, :], pBi[:, :], ct_[:, :], ALU.mult, [rBi, r_ct], [m[2][1]])
            tt(m[3][0][:, :], pBr[:, :], st_[:, :], ALU.mult, [rBr, r_st], [m[3][1]])
            tt(wre[:, :], m[0][0][:, :], m[1][0][:, :], ALU.add, [m[0][1], m[1][1]], [r_wre], eng="dve")
            tt(wim[:, :], m[2][0][:, :], m[3][0][:, :], ALU.subtract, [m[2][1], m[3][1]], [r_wim], eng="dve")
            act(cz[:, 2:3], car_r[:, st:st + 1], AF.Identity, [r_car_r, r_mag, r_wre], [r_cz],
                scale=mag[:, st:st + 1], bias=wre[:, 0:1])
            act(cz[:, 3:4], car_i[:, st:st + 1], AF.Identity, [r_car_i, r_mag, r_wim], [r_cz],
                scale=mag[:, st:st + 1], bias=wim[:, 0:1])
            act(wre[:, 0:1], cz[:, 2:3], AF.Copy, [r_cz], [r_wre])
            act(wim[:, 0:1], cz[:, 3:4], AF.Copy, [r_cz], [r_wim])
            S.op("dve", lambda e, rt_=rt_, st=st: e.tensor_tensor_scan(out=zre[:, :], data0=rt_[:, :], data1=wre[:, :],
                                                                       initial=0.0, op0=ALU.mult,
                                                                       op1=ALU.add),
                 reads=[r_rt, r_wre], writes=[r_zre])
            S.op("dve", lambda e, rt_=rt_, st=st: e.tensor_tensor_scan(out=zim[:, :], data0=rt_[:, :], data1=wim[:, :],
                                                                       initial=0.0, op0=ALU.mult,
                                                                       op1=ALU.add),
                 reads=[r_rt, r_wim], writes=[r_zim])
            L = slice(TT - 1, TT)
            s1 = slice(st, st + 1)
            nPTi = pw_ni[:, 9 * NST:10 * NST]
            act(cz[:, 0:1], zim[:, L], AF.Identity, [r_zim, r_pwni], [r_cz], scale=nPTi[:, s1])
            act(cz[:, 1:2], zim[:, L], AF.Identity, [r_zim, r_pwr], [r_cz], scale=PTr[:, s1])
            act(car_r[:, s1], zre[:, L], AF.Identity, [r_zre, r_pwr, r_cz], [r_car_r], scale=PTr[:, s1], bias=cz[:, 0:1])
            act(car_i[:, s1], zre[:, L], AF.Identity, [r_zre, r_pwi, r_cz], [r_car_i], scale=PTi[:, s1], bias=cz[:, 1:2])
            tt(m[0][0][:, :], zre[:, :], ct_[:, :], ALU.mult, [r_zre, r_ct], [m[0][1]], eng="dve")
            tt(m[1][0][:, :], zim[:, :], st_[:, :], ALU.mult, [r_zim, r_st], [m[1][1]], eng="dve")
            tt(m[2][0][:, :], zre[:, :], st_[:, :], ALU.mult, [r_zre, r_st], [m[2][1]], eng="dve")
            tt(m[3][0][:, :], zim[:, :], ct_[:, :], ALU.mult, [r_zim, r_ct], [m[3][1]], eng="dve")
            tt(xrb[:, st, :], m[0][0][:, :], m[1][0][:, :], ALU.subtract, [m[0][1], m[1][1]], [r_xrb], eng="dve")
            tt(xib[:, st, :], m[2][0][:, :], m[3][0][:, :], ALU.add, [m[2][1], m[3][1]], [r_xib], eng="dve")
        for ct in range(2):
            pY, rY = cx.bank(0, 8)
            pairs = []
            rd = []
            for st in range(ct * 4, ct * 4 + 4):
                pairs.append((LC[st][0][0][:, :], xrb[:, st, :]))
                pairs.append((LC[st][1][0][:, :], xib[:, st, :]))
                rd += [LC[st][0][1], LC[st][1][1]]
            mm_group(S, pY[:, :], rY, pairs, reads=rd + [r_xrb, r_xib])
            act(ypre[:, :], pY[:, :], AF.Copy, [rY], [r_ypre])
            stt(yss[:, :], u32[:, ct, :], d_t[:, ct:ct + 1], ypre[:, :], ALU.mult, ALU.add, [r_u32, r_d, r_ypre], [r_yss])
            gelu_tanh_mul(cx, yss[:, :], r_yss, None, None, yout[:, ct, :], r_yout, (t1, r_t1, t2, r_t2))
        store_T(cx, y_o, tok0, yout, r_yout, 2)
    S.finish("sp")
    S.emit()
    return cx.nc


def s5_inputs(W, half, cast):
    w_in = W["od_w_in"][0]
    g0 = 16 * half
    gs = slice(g0, g0 + 16)
    c = lambda a: np.ascontiguousarray(cast(a))

    def per_state(a):
        return np.ascontiguousarray(a.reshape(8, 128).T)

    lr = per_state(W["s5_a_re"][0][gs])
    li = per_state(W["s5_a_im"][0][gs])
    ldt = per_state(np.repeat(W["s5_log_dt"][0][gs][:, None], 64, 1))
    bre = np.ascontiguousarray(W["s5_b_re"][0][gs].reshape(8, 128, 16).transpose(1, 0, 2).reshape(128, 128))
    bim = np.ascontiguousarray(W["s5_b_im"][0][gs].reshape(8, 128, 16).transpose(1, 0, 2).reshape(128, 128))
    lc_re = np.zeros((8, 128, 128), np.float32)
    lc_im = np.zeros((8, 128, 128), np.float32)
    for st in range(8):
        for j in range(2):
            g = 2 * st + j
            gl = g % 8
            lc_re[st, j * 64:(j + 1) * 64, gl * 16:(gl + 1) * 16] = W["s5_c_re"][0][g0 + g].T
            lc_im[st, j * 64:(j + 1) * 64, gl * 16:(gl + 1) * 16] = W["s5_c_im"][0][g0 + g].T
    return dict(w_u=c(w_in[:, 1024 + 256 * half:1024 + 256 * half + 256]), lr=lr, li=li, ldt=ldt, b_re=bre, b_im=bim,
                lc_re=lc_re.reshape(1024, 128), lc_im=lc_im.reshape(1024, 128),
                d=np.ascontiguousarray(W["s5_d"][0][gs].reshape(256)))


BIG = ["ffn_a_w1", "ffn_a_w3", "ffn_a_w2", "ffn_b_w1", "ffn_b_w3", "ffn_b_w2", "ple_w_gate", "ple_w_up",
       "ev_w_in", "mla_w_q_up", "mla_w_kv_up", "gla_w_gate_up", "ev_w_out", "od_w_in", "rg_w_a", "rg_w_i",
       "s5_w_glu", "od_w_out"]


def _run(nc, in_maps):
    res = run_bass_kernel_spmd(nc, in_maps, core_ids=list(range(NCORES)))
    return res.results


def _ident(a):
    return a


def kernel(**inp):
    inp = {k: np.asarray(v) for k, v in inp.items()}
    x = inp["x"]
    p = inp["p"]
    B = x.shape[0]
    flat = {n: np.ascontiguousarray(inp[n]).reshape(-1) for n in BIG}
    sizes = [flat[n].size // NCORES for n in BIG]
    nc0 = build_cast(sizes)
    maps = [{f"w{i}": flat[n][c * sz:(c + 1) * sz].reshape(-1, 512) for i, (n, sz) in enumerate(zip(BIG, sizes))}
            for c in range(NCORES)]
    r0 = _run(nc0, maps)
    W = {k: v for k, v in inp.items() if k not in ("x", "p")}
    for i, n in enumerate(BIG):
        W[n] = np.concatenate([np.asarray(r0[c][f"o{i}"]).reshape(-1) for c in range(NCORES)]).reshape(inp[n].shape)

    def tok(c):
        b, half = divmod(c, 2)
        return b, slice(half * TOK, (half + 1) * TOK)

    cc = np.ascontiguousarray
    maps = []
    for c in range(NCORES):
        b, ts_ = tok(c)
        maps.append(dict(x=cc(x[b, ts_]), w1=W["ffn_a_w1"][0], w3=W["ffn_a_w3"][0], w2=W["ffn_a_w2"][0],
                         gA=W["ffn_a_norm"][0], gM=W["mix_norm"][0]))
    r1 = _run(build_L1(), maps)
    hT = [np.asarray(r1[c]["hT"]) for c in range(NCORES)]
    xnT = [np.asarray(r1[c]["xnT"]) for c in range(NCORES)]

    def full_seq(parts):
        return [cc(np.concatenate([parts[2 * b], parts[2 * b + 1]], axis=1)) for b in range(B)]

    def mixer(build_a, in_a, build_b, in_b, xn_full):
        ma, mb = [], []
        for c in range(NCORES):
            b, half = divmod(c, 2)
            da = in_a(W, half, _ident)
            da["xnT"] = xn_full[b]
            ma.append(da)
            db = in_b(W, half, _ident)
            db["xnT"] = xn_full[b]
            mb.append(db)
        ra = _run(build_a(), ma)
        rb = _run(build_b(), mb)
        ycat = []
        for b in range(B):
            ycat.append(np.concatenate([np.asarray(ra[2 * b]["yT"]), np.asarray(ra[2 * b + 1]["yT"]),
                                        np.asarray(rb[2 * b]["yT"]), np.asarray(rb[2 * b + 1]["yT"])], axis=0))
        return ycat

    ycat = mixer(build_mla, mla_inputs, build_gla, gla_inputs, full_seq(xnT))
    maps = []
    for c in range(NCORES):
        b, ts_ = tok(c)
        maps.append(dict(hT_in=hT[c], yT_in=cc(ycat[b][:, ts_]), p=cc(p[0, b, ts_]), w_out=W["ev_w_out"][0],
                         w1=W["ffn_b_w1"][0], w3=W["ffn_b_w3"][0], w2=W["ffn_b_w2"][0], gB=W["ffn_b_norm"][0],
                         gP=W["ple_norm"][0], w_gate=W["ple_w_gate"][0], w_up=W["ple_w_up"][0],
                         n_w1=W["ffn_a_w1"][1], n_w3=W["ffn_a_w3"][1], n_w2=W["ffn_a_w2"][1],
                         gA=W["ffn_a_norm"][1], gM=W["mix_norm"][1]))
    r3 = _run(build_row(False), maps)
    hT = [np.asarray(r3[c]["hT"]) for c in range(NCORES)]
    xnT = [np.asarray(r3[c]["xnT"]) for c in range(NCORES)]
    ycat = mixer(build_rglru, rglru_inputs, build_s5, s5_inputs, full_seq(xnT))
    maps = []
    for c in range(NCORES):
        b, ts_ = tok(c)
        maps.append(dict(hT_in=hT[c], yT_in=cc(ycat[b][:, ts_]), p=cc(p[1, b, ts_]), w_out=W["od_w_out"][0],
                         w1=W["ffn_b_w1"][1], w3=W["ffn_b_w3"][1], w2=W["ffn_b_w2"][1], gB=W["ffn_b_norm"][1],
                         gP=W["ple_norm"][1], w_gate=W["ple_w_gate"][1], w_up=W["ple_w_up"][1],
                         w_glu=W["s5_w_glu"][0], b_glu=W["s5_b_glu"][0], gF=W["final_norm"]))
    r5 = _run(build_row(True), maps)
    out = np.empty(x.shape, np.float32)
    for c in range(NCORES):
        b, ts_ = tok(c)
        out[b, ts_] = np.asarray(r5[c]["out"])
    return out
```

```python
import numpy as np
import ml_dtypes
import concourse.bass as bass
import concourse.mybir as mybir
from concourse.bass_utils import run_bass_kernel_spmd

F32 = mybir.dt.float32
BF16 = mybir.dt.bfloat16
I32 = mybir.dt.int32
AF = mybir.ActivationFunctionType
ALU = mybir.AluOpType
NPBF = ml_dtypes.bfloat16

NCORES = 8
D = 1024
DFF = 2816
SEQ = 8192
TOK = 4096
TT = 512
KC = D // 128
FC = DFF // 128
EPS = 1e-6


class Res:
    __slots__ = ("name", "writer", "readers")

    def __init__(self, name):
        self.name = name
        self.writer = None
        self.readers = []


class Sched:
    ENGS = ("pe", "act", "dve", "pool", "sp")

    def __init__(self, nc, n_dma_sems=40):
        self.nc = nc
        self.q = {e: [] for e in self.ENGS}
        self.sem = {e: nc.alloc_semaphore("prog_" + e) for e in self.ENGS}
        self.cnt = {e: 0 for e in self.ENGS}
        self.waited = {e: {} for e in self.ENGS}
        self.dsems = [nc.alloc_semaphore(f"dma{i}") for i in range(n_dma_sems)]
        self.dcnt = [0] * n_dma_sems
        self.dnext = 0
        self.semname = {}
        self.all_dma_tokens = []

    def _need(self, eng, waits, tok):
        if tok is None:
            return
        s, v = tok
        key = id(s)
        if self.waited[eng].get(key, 0) >= v:
            return
        if key in waits and waits[key][1] >= v:
            return
        waits[key] = (s, v)

    def _deps(self, eng, reads, writes):
        waits = {}
        for r in reads:
            self._need(eng, waits, r.writer)
        for w in writes:
            self._need(eng, waits, w.writer)
            for t in w.readers:
                self._need(eng, waits, t)
        return waits

    def _commit(self, eng, waits, tok, reads, writes):
        for s, v in waits.values():
            self.waited[eng][id(s)] = max(self.waited[eng].get(id(s), 0), v)
        for r in reads:
            r.readers.append(tok)
        for w in writes:
            w.writer = tok
            w.readers = []

    def op(self, eng, fn, reads=(), writes=()):
        waits = self._deps(eng, reads, writes)
        self.cnt[eng] += 1
        tok = (self.sem[eng], self.cnt[eng])
        self._commit(eng, waits, tok, reads, writes)
        self.q[eng].append((list(waits.values()), fn, (self.sem[eng], 1)))
        return tok

    def raw(self, eng, fn):
        self.q[eng].append(([], fn, None))

    def dma(self, eng, fn, reads=(), writes=()):
        waits = self._deps(eng, reads, writes)
        i = self.dnext
        self.dnext = (self.dnext + 1) % len(self.dsems)
        s = self.dsems[i]
        if self.dcnt[i] > 0:
            self._need(eng, waits, (s, self.dcnt[i]))
        self.dcnt[i] += 16
        tok = (s, self.dcnt[i])
        self._commit(eng, waits, tok, reads, writes)
        self.q[eng].append((list(waits.values()), fn, (s, 16)))
        return tok

    def barrier(self):
        waits = []
        for i, s in enumerate(self.dsems):
            if self.dcnt[i] > 0:
                waits.append((s, self.dcnt[i]))
        for e in self.ENGS:
            if self.cnt[e] > 0:
                waits.append((self.sem[e], self.cnt[e]))
        for e in self.ENGS:
            self.q[e].append(([w for w in waits if w[0] is not self.sem[e]], None, None))
            for s_, v_ in waits:
                self.waited[e][id(s_)] = max(self.waited[e].get(id(s_), 0), v_)

    def finish(self, eng="sp"):
        waits = []
        for i, s in enumerate(self.dsems):
            if self.dcnt[i] > 0:
                waits.append((s, self.dcnt[i]))
        for e in self.ENGS:
            if self.cnt[e] > 0 and e != eng:
                waits.append((self.sem[e], self.cnt[e]))
        self.q[eng].append((waits, None, None))

    def emit(self):
        nc = self.nc
        q = self.q

        def replay(e, eng):
            for waits, fn, inc in q[e]:
                for s, v in waits:
                    eng.wait_ge(s, v)
                if fn is None:
                    continue
                ins = fn(eng)
                if inc is not None:
                    ins.then_inc(inc[0], inc[1])

        with nc.Block() as block:
            @block.tensor
            def _(eng):
                replay("pe", eng)

            @block.scalar
            def _(eng):
                replay("act", eng)

            @block.vector
            def _(eng):
                replay("dve", eng)

            @block.gpsimd
            def _(eng):
                replay("pool", eng)

            @block.sync
            def _(eng):
                replay("sp", eng)


class Ctx:
    def __init__(self, bf_bank=False):
        self.nc = bass.Bass("TRN2", target_bir_lowering=False)
        self.S = Sched(self.nc)
        self.n = 0
        self.psum = []
        for i in range(8):
            if bf_bank and i == 7:
                t = self.nc.alloc_psum_tensor(f"ps{i}", [128, 1024], BF16)
            else:
                t = self.nc.alloc_psum_tensor(f"ps{i}", [128, 512], F32)
            self.psum.append((t, Res(f"ps{i}")))
        self.ps_next = 0

    def name(self, p):
        self.n += 1
        return f"{p}_{self.n}"

    def sb(self, shape, dtype, name="t"):
        t = self.nc.alloc_sbuf_tensor(self.name(name), list(shape), dtype)
        return t, Res(name)

    def din(self, name, shape, dtype):
        return self.nc.dram_tensor(name, list(shape), dtype, kind="ExternalInput").ap()

    def dout(self, name, shape, dtype):
        return self.nc.dram_tensor(name, list(shape), dtype, kind="ExternalOutput").ap()

    def bank(self, lo=0, hi=8):
        i = lo + (self.ps_next % (hi - lo))
        self.ps_next += 1
        return self.psum[i]


def mm_group(S, out_ap, out_res, pairs, reads, start=True, stop=True):
    n = len(pairs)
    tok = None
    for i, (l, r) in enumerate(pairs):
        st = start and i == 0
        sp = stop and i == n - 1
        fn = (lambda e, l=l, r=r, st=st, sp=sp: e.matmul(out_ap, l, r, start=st, stop=sp))
        if i == 0 and n > 1:
            w = S._deps("pe", reads, [out_res])
            for s_, v_ in w.values():
                S.waited["pe"][id(s_)] = max(S.waited["pe"].get(id(s_), 0), v_)
            S.q["pe"].append((list(w.values()), fn, None))
        elif i == n - 1:
            tok = S.op("pe", fn, reads=reads, writes=[out_res])
        else:
            S.raw("pe", fn)
    return tok


class Consts:
    def __init__(self, cx):
        S = cx.S
        self.ones_bf, self.r_ones = cx.sb([128, 128], BF16, "ones")
        S.op("pool", lambda e: e.memset(self.ones_bf[:, :], 1.0), writes=[self.r_ones])
        self.ident, self.r_ident = cx.sb([128, 128], F32, "ident")
        S.op("pool", lambda e: e.memset(self.ident[:, :], 0.0), writes=[self.r_ident])
        S.op("pool", lambda e: e.affine_select(out=self.ident[:, :], in_=self.ident[:, :],
                                               pattern=[[1, 128]], compare_op=ALU.not_equal,
                                               fill=1.0, base=0, channel_multiplier=-1),
             reads=[self.r_ident], writes=[self.r_ident])


def load_vec_cols(cx, dram_vec_ap, n, name):
    t, r = cx.sb([128, n], F32, name)
    cx.S.dma("sp", lambda e: e.dma_start(out=t[:, :], in_=dram_vec_ap.rearrange("(c p) -> p c", p=128),
                                         allow_slow_non_contiguous=True), writes=[r])
    return t, r


def rmsnorm_T(cx, C, hT, r_h, nch, g, r_g, outT, r_out, nfeat, scratch, width=TT):
    S = cx.S
    sq, r_sq, rstd, r_rstd = scratch
    bank, r_bank = cx.bank(4, 8)
    pairs = []
    for c in range(nch):
        S.op("act", lambda e, c=c: e.activation(out=sq[:, c, :width], in_=hT[:, c, :width], func=AF.Square),
             reads=[r_h], writes=[r_sq])
        pairs.append((C.ones_bf[:, :], sq[:, c, :width]))
    mm_group(S, bank[:, :width], r_bank, pairs, reads=[r_sq, C.r_ones])
    S.op("act", lambda e: e.activation(out=rstd[:, :width], in_=bank[:, :width], func=AF.Sqrt,
                                       scale=1.0 / nfeat, bias=EPS),
         reads=[r_bank], writes=[r_rstd])
    S.op("dve", lambda e: e.reciprocal(out=rstd[:, :width], in_=rstd[:, :width]),
         reads=[r_rstd], writes=[r_rstd])
    for c in range(nch):
        S.op("dve", lambda e, c=c: e.scalar_tensor_tensor(out=outT[:, c, :width], in0=hT[:, c, :width],
                                                          scalar=g[:, c:c + 1], in1=rstd[:, :width],
                                                          op0=ALU.mult, op1=ALU.mult),
             reads=[r_h, r_g, r_rstd], writes=[r_out])


def pe_group(S, fns, reads, out_res):
    n = len(fns)
    tok = None
    for i, fn in enumerate(fns):
        if i == n - 1:
            tok = S.op("pe", fn, reads=reads, writes=[out_res])
        elif i == 0:
            w = S._deps("pe", reads, [out_res])
            for s_, v_ in w.values():
                S.waited["pe"][id(s_)] = max(S.waited["pe"].get(id(s_), 0), v_)
            S.q["pe"].append((list(w.values()), fn, None))
        else:
            S.raw("pe", fn)
    return tok


class WStream:
    def __init__(self, cx, kch, width, nbuf, name):
        self.slabs = [cx.sb([128, kch, width], BF16, name) for _ in range(nbuf)]
        self.i = 0

    def load(self, cx, w_ap, col0, width, eng="sp"):
        t, r = self.slabs[self.i % len(self.slabs)]
        self.i += 1
        kch = w_ap.shape[0] // 128
        src = w_ap.rearrange("(c p) f -> p c f", p=128)[:, :, col0:col0 + width]
        cx.S.dma(eng, lambda e: e.dma_start(out=t[:, :kch, :width], in_=src), writes=[r])
        return t, r


class RowState:
    def __init__(self, cx):
        self.hT, self.r_h = cx.sb([128, KC, TT], F32, "hT")
        self.xn, self.r_xn = cx.sb([128, KC, TT], BF16, "xn")
        self.sq, self.r_sq = cx.sb([128, KC, TT], BF16, "sq")
        self.rstd, self.r_rstd = cx.sb([128, TT], F32, "rstd")
        self.gT, _ = cx.sb([128, FC, TT], BF16, "gT")
        self.r_g = [Res(f"g{j}") for j in range(FC)]
        self.sl = [cx.sb([128, TT], F32, "sl") for _ in range(2)]
        self.sli = 0
        self.ws13 = WStream(cx, KC, 256, 4, "ws13")
        self.ws2 = WStream(cx, FC, 256, 2, "ws2")

    def scratch(self):
        return (self.sq, self.r_sq, self.rstd, self.r_rstd)


def ffn_T(cx, C, st, w1, w3, w2, g, r_g):
    S = cx.S
    rmsnorm_T(cx, C, st.hT, st.r_h, KC, g, r_g, st.xn, st.r_xn, D, st.scratch())
    for jb in range(FC // 2):
        a, ra = st.ws13.load(cx, w1, jb * 256, 256)
        b, rb = st.ws13.load(cx, w3, jb * 256, 256)
        for jj in range(2):
            j = jb * 2 + jj
            pA, rA = cx.bank(0, 4)
            pB, rB = cx.bank(0, 4)
            mm_group(S, pA[:, :], rA, [(a[:, k, jj * 128:(jj + 1) * 128], st.xn[:, k, :]) for k in range(KC)],
                     reads=[ra, st.r_xn])
            mm_group(S, pB[:, :], rB, [(b[:, k, jj * 128:(jj + 1) * 128], st.xn[:, k, :]) for k in range(KC)],
                     reads=[rb, st.r_xn])
            sl, r_sl = st.sl[st.sli % 2]
            st.sli += 1
            S.op("act", lambda e, sl=sl, pA=pA: e.activation(out=sl[:, :], in_=pA[:, :], func=AF.Silu),
                 reads=[rA], writes=[r_sl])
            S.op("dve", lambda e, sl=sl, pB=pB, j=j: e.tensor_tensor(out=st.gT[:, j, :], in0=pB[:, :], in1=sl[:, :],
                                                                     op=ALU.mult),
                 reads=[rB, r_sl], writes=[st.r_g[j]])
    for mb in range(KC // 2):
        c, rc = st.ws2.load(cx, w2, mb * 256, 256)
        for mm in range(2):
            m = mb * 2 + mm
            pO, rO = cx.bank(4, 8)
            mm_group(S, pO[:, :], rO, [(c[:, j, mm * 128:(mm + 1) * 128], st.gT[:, j, :]) for j in range(FC)],
                     reads=[rc] + st.r_g)
            S.op("dve", lambda e, pO=pO, m=m: e.scalar_tensor_tensor(out=st.hT[:, m, :], in0=pO[:, :], scalar=0.5,
                                                                     in1=st.hT[:, m, :], op0=ALU.mult, op1=ALU.add),
                 reads=[rO, st.r_h], writes=[st.r_h])


def load_tokmajor_T(cx, C, src_ap, tok0, nfeat_ch, xin, r_xin, dstT, r_dst, dst_is_bf16=False):
    S = cx.S
    S.dma("sp", lambda e: e.dma_start(out=xin[:, :, :nfeat_ch * 128],
                                      in_=src_ap[tok0:tok0 + TT, :].rearrange("(b p) d -> p b d", p=128)),
          writes=[r_xin])
    for c in range(nfeat_ch):
        bank, rb = cx.bank(4, 8)
        fns = [(lambda e, b=b, c=c, bank=bank: e.transpose(out=bank[:, b * 128:(b + 1) * 128],
                                                             in_=xin[:, b, c * 128:(c + 1) * 128],
                                                             identity=C.ident[:, :])) for b in range(TT // 128)]
        pe_group(S, fns, reads=[r_xin, C.r_ident], out_res=rb)
        S.op("act", lambda e, c=c, bank=bank: e.activation(out=dstT[:, c, :], in_=bank[:, :], func=AF.Copy),
             reads=[rb], writes=[r_dst])


def store_T(cx, dst_ap, tok0, srcT, r_src, nch, width=TT, eng="sp"):
    cx.S.dma(eng, lambda e: e.dma_start(out=dst_ap.rearrange("(c p) t -> p c t", p=128)[:, :, tok0:tok0 + width],
                                        in_=srcT[:, :nch, :width]), reads=[r_src])


def load_T(cx, src_ap, tok0, dstT, r_dst, nch, width=TT, eng="sp"):
    cx.S.dma(eng, lambda e: e.dma_start(out=dstT[:, :nch, :width],
                                        in_=src_ap.rearrange("(c p) t -> p c t", p=128)[:, :, tok0:tok0 + width]),
             writes=[r_dst])


def build_cast(sizes):
    cx = Ctx()
    S = cx.S
    for i, n in enumerate(sizes):
        rows = n // 512
        src = cx.din(f"w{i}", [rows, 512], F32)
        dst = cx.dout(f"o{i}", [rows, 512], BF16)
        step = 512
        for r0 in range(0, rows, step):
            r1 = min(rows, r0 + step)
            S.dma("pool", lambda e, r0=r0, r1=r1, src=src, dst=dst: e.dma_start(out=dst[r0:r1, :], in_=src[r0:r1, :]))
    S.finish("pool")
    S.emit()
    return cx.nc


def build_L1(ntiles=TOK // TT):
    cx = Ctx()
    S = cx.S
    x = cx.din("x", [TOK, D], F32)
    w1 = cx.din("w1", [D, DFF], BF16)
    w3 = cx.din("w3", [D, DFF], BF16)
    w2 = cx.din("w2", [DFF, D], BF16)
    gA = cx.din("gA", [D], F32)
    gM = cx.din("gM", [D], F32)
    hT_o = cx.dout("hT", [D, TOK], F32)
    xn_o = cx.dout("xnT", [D, TOK], BF16)
    C = Consts(cx)
    st = RowState(cx)
    xin, r_xin = cx.sb([128, TT // 128, D], F32, "xin")
    xn2, r_xn2 = cx.sb([128, KC, TT], BF16, "xn2")
    gA_t, r_gA = load_vec_cols(cx, gA, KC, "gA")
    gM_t, r_gM = load_vec_cols(cx, gM, KC, "gM")
    for t in range(ntiles):
        tok0 = t * TT
        load_tokmajor_T(cx, C, x, tok0, KC, xin, r_xin, st.hT, st.r_h)
        ffn_T(cx, C, st, w1, w3, w2, gA_t, r_gA)
        rmsnorm_T(cx, C, st.hT, st.r_h, KC, gM_t, r_gM, xn2, r_xn2, D, st.scratch())
        store_T(cx, xn_o, tok0, xn2, r_xn2, KC)
        store_T(cx, hT_o, tok0, st.hT, st.r_h, KC)
    S.finish("sp")
    S.emit()
    return cx.nc


def outproj_T(cx, st, w_out, ysrc, r_ysrc):
    S = cx.S
    for mb in range(KC // 2):
        a, ra = st.ws13.load(cx, w_out, mb * 256, 256)
        for mm in range(2):
            m = mb * 2 + mm
            pO, rO = cx.bank(4, 8)
            mm_group(S, pO[:, :], rO, [(a[:, k, mm * 128:(mm + 1) * 128], ysrc[k]) for k in range(KC)],
                     reads=[ra] + r_ysrc)
            S.op("dve", lambda e, pO=pO, m=m: e.tensor_tensor(out=st.hT[:, m, :], in0=pO[:, :], in1=st.hT[:, m, :],
                                                              op=ALU.add),
                 reads=[rO, st.r_h], writes=[st.r_h])


def ple_T(cx, C, st, w_gate, wup_t, r_wup, pT, r_pT, g, r_g):
    S = cx.S
    rmsnorm_T(cx, C, st.hT, st.r_h, KC, g, r_g, st.xn, st.r_xn, D, st.scratch())
    for mb in range(KC // 2):
        a, ra = st.ws13.load(cx, w_gate, mb * 256, 256)
        for mm in range(2):
            m = mb * 2 + mm
            pG, rG = cx.bank(4, 8)
            pU, rU = cx.bank(4, 8)
            mm_group(S, pG[:, :], rG, [(a[:, k, mm * 128:(mm + 1) * 128], st.xn[:, k, :]) for k in range(KC)],
                     reads=[ra, st.r_xn])
            mm_group(S, pU[:, :], rU, [(wup_t[:, k, m * 128:(m + 1) * 128], pT[:, k, :]) for k in range(2)],
                     reads=[r_wup, r_pT])
            sl, r_sl = st.sl[st.sli % 2]
            st.sli += 1
            S.op("act", lambda e, sl=sl, pG=pG: e.activation(out=sl[:, :], in_=pG[:, :], func=AF.Sigmoid),
                 reads=[rG], writes=[r_sl])
            S.op("dve", lambda e, sl=sl, pU=pU: e.tensor_tensor(out=sl[:, :], in0=pU[:, :], in1=sl[:, :], op=ALU.mult),
                 reads=[rU, r_sl], writes=[r_sl])
            S.op("dve", lambda e, sl=sl, m=m: e.tensor_tensor(out=st.hT[:, m, :], in0=sl[:, :], in1=st.hT[:, m, :],
                                                              op=ALU.add),
                 reads=[r_sl, st.r_h], writes=[st.r_h])


def load_resident(cx, w_ap, name):
    K_, M_ = w_ap.shape
    t, r = cx.sb([128, K_ // 128, M_], BF16, name)
    cx.S.dma("sp", lambda e: e.dma_start(out=t[:, :, :], in_=w_ap.rearrange("(c p) f -> p c f", p=128)), writes=[r])
    return t, r


def build_row(last, ntiles=TOK // TT):
    cx = Ctx()
    S = cx.S
    hT_i = cx.din("hT_in", [D, TOK], F32)
    yT_i = cx.din("yT_in", [D, TOK], BF16)
    p_i = cx.din("p", [TOK, 256], F32)
    w_out = cx.din("w_out", [D, D], BF16)
    w1 = cx.din("w1", [D, DFF], BF16)
    w3 = cx.din("w3", [D, DFF], BF16)
    w2 = cx.din("w2", [DFF, D], BF16)
    gB = cx.din("gB", [D], F32)
    gP = cx.din("gP", [D], F32)
    w_gate = cx.din("w_gate", [D, D], BF16)
    w_up = cx.din("w_up", [256, D], BF16)
    if last:
        w_glu = cx.din("w_glu", [512, 512], BF16)
        b_glu = cx.din("b_glu", [512], F32)
        gF = cx.din("gF", [D], F32)
        out_o = cx.dout("out", [TOK, D], F32)
    else:
        n1 = cx.din("n_w1", [D, DFF], BF16)
        n3 = cx.din("n_w3", [D, DFF], BF16)
        n2 = cx.din("n_w2", [DFF, D], BF16)
        gA = cx.din("gA", [D], F32)
        gM = cx.din("gM", [D], F32)
        hT_o = cx.dout("hT", [D, TOK], F32)
        xn_o = cx.dout("xnT", [D, TOK], BF16)
    C = Consts(cx)
    st = RowState(cx)
    xin, r_xin = cx.sb([128, TT // 128, D], F32, "xin")
    yin, r_yin = cx.sb([128, KC, TT], BF16, "yin")
    pT, r_pT = cx.sb([128, 2, TT], BF16, "pT")
    xn2, r_xn2 = cx.sb([128, KC, TT], F32 if last else BF16, "xn2")
    wup_t, r_wup = load_resident(cx, w_up, "wup")
    gB_t, r_gB = load_vec_cols(cx, gB, KC, "gB")
    gP_t, r_gP = load_vec_cols(cx, gP, KC, "gP")
    if last:
        wglu_t, r_wglu = load_resident(cx, w_glu, "wglu")
        bglu_t, r_bglu = load_vec_cols(cx, b_glu, 4, "bglu")
        gF_t, r_gF = load_vec_cols(cx, gF, KC, "gF")
        yg, r_yg = cx.sb([128, 4, TT], BF16, "yg")
    else:
        gA_t, r_gA = load_vec_cols(cx, gA, KC, "gA")
        gM_t, r_gM = load_vec_cols(cx, gM, KC, "gM")
    for t in range(ntiles):
        tok0 = t * TT
        load_T(cx, hT_i, tok0, st.hT, st.r_h, KC)
        load_T(cx, yT_i, tok0, yin, r_yin, KC)
        load_tokmajor_T(cx, C, p_i, tok0, 2, xin, r_xin, pT, r_pT)
        if last:
            for m in range(4):
                pZ, rZ = cx.bank(4, 8)
                mm_group(S, pZ[:, :], rZ, [(wglu_t[:, k, m * 128:(m + 1) * 128], yin[:, 4 + k, :]) for k in range(4)],
                         reads=[r_wglu, r_yin])
                sl, r_sl = st.sl[st.sli % 2]
                st.sli += 1
                S.op("act", lambda e, sl=sl, pZ=pZ, m=m: e.activation(out=sl[:, :], in_=pZ[:, :], func=AF.Sigmoid,
                                                                      bias=bglu_t[:, m:m + 1]),
                     reads=[rZ, r_bglu], writes=[r_sl])
                S.op("dve", lambda e, sl=sl, m=m: e.tensor_tensor(out=yg[:, m, :], in0=yin[:, 4 + m, :], in1=sl[:, :],
                                                                  op=ALU.mult),
                     reads=[r_yin, r_sl], writes=[r_yg])
            ysrc = [yin[:, k, :] for k in range(4)] + [yg[:, k, :] for k in range(4)]
            outproj_T(cx, st, w_out, ysrc, [r_yin, r_yg])
        else:
            outproj_T(cx, st, w_out, [yin[:, k, :] for k in range(KC)], [r_yin])
        ffn_T(cx, C, st, w1, w3, w2, gB_t, r_gB)
        ple_T(cx, C, st, w_gate, wup_t, r_wup, pT, r_pT, gP_t, r_gP)
        if last:
            rmsnorm_T(cx, C, st.hT, st.r_h, KC, gF_t, r_gF, xn2, r_xn2, D, st.scratch())
            for b in range(TT // 128):
                for half in range(2):
                    bank, rb = cx.bank(4, 8)
                    fns = [(lambda e, b=b, c=c, half=half, bank=bank: e.transpose(
                        out=bank[:, c * 128:(c + 1) * 128], in_=xn2[:, half * 4 + c, b * 128:(b + 1) * 128],
                        identity=C.ident[:, :])) for c in range(4)]
                    pe_group(S, fns, reads=[r_xn2, C.r_ident], out_res=rb)
                    S.op("act", lambda e, b=b, half=half, bank=bank: e.activation(
                        out=xin[:, b, half * 512:(half + 1) * 512], in_=bank[:, :], func=AF.Copy),
                         reads=[rb], writes=[r_xin])
            S.dma("sp", lambda e, tok0=tok0: e.dma_start(
                out=out_o[tok0:tok0 + TT, :].rearrange("(b p) d -> p b d", p=128), in_=xin[:, :, :]), reads=[r_xin])
        else:
            ffn_T(cx, C, st, n1, n3, n2, gA_t, r_gA)
            rmsnorm_T(cx, C, st.hT, st.r_h, KC, gM_t, r_gM, xn2, r_xn2, D, st.scratch())
            store_T(cx, xn_o, tok0, xn2, r_xn2, KC)
            store_T(cx, hT_o, tok0, st.hT, st.r_h, KC)
    S.finish("sp")
    S.emit()
    return cx.nc


NQT = SEQ // TT
SCALE = 96 ** -0.5


def build_mla(ntiles=NQT):
    cx = Ctx()
    S = cx.S
    xnT = cx.din("xnT", [D, SEQ], BF16)
    w_cq = cx.din("w_cq", [D, 384], BF16)
    w_ckv = cx.din("w_ckv", [D, 256], BF16)
    w_kr = cx.din("w_kr", [D, 128], BF16)
    w_kr2 = cx.din("w_kr2", [D, 128], BF16)
    wq = cx.din("wq", [384, 512], BF16)
    wq2 = cx.din("wq2", [384, 512], BF16)
    wk = cx.din("wk", [256, 256], BF16)
    wv = cx.din("wv", [256, 256], BF16)
    qn = cx.din("q_norm", [384], F32)
    kvn = cx.din("kv_norm", [256], F32)
    cos_d = cx.din("cos", [32, SEQ], F32)
    sin_d = cx.din("sin", [32, SEQ], F32)
    y_o = cx.dout("yT", [256, SEQ], BF16)
    C = Consts(cx)
    wcq_t, r_wcq = load_resident(cx, w_cq, "wcq")
    wckv_t, r_wckv = load_resident(cx, w_ckv, "wckv")
    wkr_t, r_wkr = load_resident(cx, w_kr, "wkr")
    wkr2_t, r_wkr2 = load_resident(cx, w_kr2, "wkr2")
    wq_t, r_wq = load_resident(cx, wq, "wq")
    wq2_t, r_wq2 = load_resident(cx, wq2, "wq2")
    wk_t, r_wk = load_resident(cx, wk, "wk")
    wv_t, r_wv = load_resident(cx, wv, "wv")
    qn_t, r_qn = load_vec_cols(cx, qn, 3, "qn")
    kvn_t, r_kvn = load_vec_cols(cx, kvn, 2, "kvn")
    KT = [cx.sb([128, SEQ], BF16, f"KT{h}") for h in range(4)]
    Vt, r_Vt = cx.sb([128, SEQ // 128, 256], BF16, "Vtok")
    xn = [cx.sb([128, KC, TT], BF16, "xn") for _ in range(2)]
    cqT, r_cqT = cx.sb([128, 3, TT], F32, "cqT")
    cqn, r_cqn = cx.sb([128, 3, TT], BF16, "cqn")
    ckvT, r_ckvT = cx.sb([128, 2, TT], F32, "ckvT")
    ckvn, r_ckvn = cx.sb([128, 2, TT], BF16, "ckvn")
    sq, r_sq = cx.sb([128, 3, TT], BF16, "sq")
    rstd, r_rstd = cx.sb([128, TT], F32, "rstd")
    scratch = (sq, r_sq, rstd, r_rstd)
    cs, r_cs = cx.sb([128, TT], F32, "cos")
    sn, r_sn = cx.sb([128, TT], F32, "sin")
    t1, r_t1 = cx.sb([128, TT], F32, "t1")
    t2, r_t2 = cx.sb([128, TT], F32, "t2")
    QT = [cx.sb([128, TT], BF16, f"QT{h}") for h in range(4)]
    PT = [cx.sb([128, TT], BF16, "PT") for _ in range(3)]
    pti = 0
    rD, r_rD = cx.sb([128, TT], F32, "rD")
    ya, r_ya = cx.sb([128, 2, TT], BF16, "ya")
    RP = slice(64, 96)
    for t in range(ntiles):
        tok0 = t * TT
        x_t, r_x = xn[t % 2]
        load_T(cx, xnT, tok0, x_t, r_x, KC)
        S.dma("sp", lambda e, tok0=tok0: e.dma_start(out=cs[RP, :], in_=cos_d[:, tok0:tok0 + TT]), writes=[r_cs])
        S.dma("sp", lambda e, tok0=tok0: e.dma_start(out=sn[RP, :], in_=sin_d[:, tok0:tok0 + TT]), writes=[r_sn])
        for m in range(3):
            b_, rb = cx.bank(5, 8)
            mm_group(S, b_[:, :], rb, [(wcq_t[:, k, m * 128:(m + 1) * 128], x_t[:, k, :]) for k in range(KC)],
                     reads=[r_wcq, r_x])
            S.op("act", lambda e, b_=b_, m=m: e.activation(out=cqT[:, m, :], in_=b_[:, :], func=AF.Copy),
                 reads=[rb], writes=[r_cqT])
        for m in range(2):
            b_, rb = cx.bank(5, 8)
            mm_group(S, b_[:, :], rb, [(wckv_t[:, k, m * 128:(m + 1) * 128], x_t[:, k, :]) for k in range(KC)],
                     reads=[r_wckv, r_x])
            S.op("act", lambda e, b_=b_, m=m: e.activation(out=ckvT[:, m, :], in_=b_[:, :], func=AF.Copy),
                 reads=[rb], writes=[r_ckvT])
        rmsnorm_T(cx, C, cqT, r_cqT, 3, qn_t, r_qn, cqn, r_cqn, 384, scratch)
        rmsnorm_T(cx, C, ckvT, r_ckvT, 2, kvn_t, r_kvn, ckvn, r_ckvn, 256, scratch)
        b1, rb1 = cx.bank(5, 8)
        mm_group(S, b1[:, :], rb1, [(wkr_t[:, k, :], x_t[:, k, :]) for k in range(KC)], reads=[r_wkr, r_x])
        b2, rb2 = cx.bank(5, 8)
        mm_group(S, b2[:, :], rb2, [(wkr2_t[:, k, :], x_t[:, k, :]) for k in range(KC)], reads=[r_wkr2, r_x])
        S.op("dve", lambda e, b1=b1: e.tensor_tensor(out=t1[RP, :], in0=b1[RP, :], in1=cs[RP, :], op=ALU.mult),
             reads=[rb1, r_cs], writes=[r_t1])
        S.op("dve", lambda e, b2=b2: e.tensor_tensor(out=t2[RP, :], in0=b2[RP, :], in1=sn[RP, :], op=ALU.mult),
             reads=[rb2, r_sn], writes=[r_t2])
        S.op("dve", lambda e: e.tensor_tensor(out=t1[RP, :], in0=t1[RP, :], in1=t2[RP, :], op=ALU.add),
             reads=[r_t1, r_t2], writes=[r_t1])
        for h in range(4):
            kt_, r_kt = KT[h]
            S.op("act", lambda e, kt_=kt_, tok0=tok0: e.activation(out=kt_[RP, tok0:tok0 + TT], in_=t1[RP, :],
                                                                    func=AF.Copy),
                 reads=[r_t1], writes=[r_kt])
            bk, rbk = cx.bank(5, 8)
            mm_group(S, bk[0:64, :], rbk, [(wk_t[:, k, h * 64:(h + 1) * 64], ckvn[:, k, :]) for k in range(2)],
                     reads=[r_wk, r_ckvn])
            S.op("act", lambda e, kt_=kt_, bk=bk, tok0=tok0: e.activation(out=kt_[0:64, tok0:tok0 + TT],
                                                                           in_=bk[0:64, :], func=AF.Copy),
                 reads=[rbk], writes=[r_kt])
        for b in range(TT // 128):
            bv, rbv = cx.bank(5, 8)
            mm_group(S, bv[:, 0:256], rbv, [(ckvn[:, k, b * 128:(b + 1) * 128], wv_t[:, k, :]) for k in range(2)],
                     reads=[r_wv, r_ckvn])
            S.op("act", lambda e, bv=bv, b=b, t=t: e.activation(out=Vt[:, t * 4 + b, :], in_=bv[:, 0:256],
                                                                func=AF.Copy),
                 reads=[rbv], writes=[r_Vt])
        for h in range(4):
            q_, r_q = QT[h]
            bq, rbq = cx.bank(5, 8)
            mm_group(S, bq[:, :], rbq, [(wq_t[:, k, h * 128:(h + 1) * 128], cqn[:, k, :]) for k in range(3)],
                     reads=[r_wq, r_cqn])
            bq2, rbq2 = cx.bank(5, 8)
            mm_group(S, bq2[:, :], rbq2, [(wq2_t[:, k, h * 128:(h + 1) * 128], cqn[:, k, :]) for k in range(3)],
                     reads=[r_wq2, r_cqn])
            S.op("act", lambda e, q_=q_, bq=bq: e.activation(out=q_[0:64, :], in_=bq[0:64, :], func=AF.Copy),
                 reads=[rbq], writes=[r_q])
            S.op("dve", lambda e, bq=bq: e.tensor_tensor(out=t1[RP, :], in0=bq[RP, :], in1=cs[RP, :], op=ALU.mult),
                 reads=[rbq, r_cs], writes=[r_t1])
            S.op("dve", lambda e, bq2=bq2: e.tensor_tensor(out=t2[RP, :], in0=bq2[RP, :], in1=sn[RP, :], op=ALU.mult),
                 reads=[rbq2, r_sn], writes=[r_t2])
            S.op("dve", lambda e, q_=q_: e.tensor_tensor(out=q_[RP, :], in0=t1[RP, :], in1=t2[RP, :], op=ALU.add),
                 reads=[r_t1, r_t2], writes=[r_q])
        for h in range(4):
            q_, r_q = QT[h]
            kt_, r_kt = KT[h]
            po = slice(0, 64) if h % 2 == 0 else slice(64, 128)
            pO, rO = cx.psum[3]
            pD, rDn = cx.psum[4]
            nk = 4 * t + 4
            LOOK = 2
            pend = {}
            for i in range(nk + LOOK):
                if i < nk:
                    kt = i
                    pS, rS = cx.bank(0, 3)
                    S.op("pe", lambda e, pS=pS, kt_=kt_, q_=q_, kt=kt: e.matmul(
                        pS[:, :], kt_[0:96, kt * 128:(kt + 1) * 128], q_[0:96, :], start=True, stop=True),
                         reads=[r_kt, r_q], writes=[rS])
                    pend[kt] = (pS, rS)
                kt = i - LOOK
                if kt < 0:
                    continue
                pS, rS = pend.pop(kt)
                p_, r_p = PT[pti % 3]
                pti += 1
                S.op("act", lambda e, p_=p_, pS=pS: e.activation(out=p_[:, :], in_=pS[:, :], func=AF.Exp, scale=SCALE),
                     reads=[rS], writes=[r_p])
                if kt >= 4 * t:
                    j = kt - 4 * t
                    S.op("pool", lambda e, p_=p_, j=j: e.affine_select(
                        out=p_[:, :], in_=p_[:, :], pattern=[[1, TT]], compare_op=ALU.is_ge, fill=0.0,
                        base=-128 * j, channel_multiplier=-1), reads=[r_p], writes=[r_p])
                S.op("pe", lambda e, p_=p_, kt=kt, h=h, nk=nk, pO=pO, po=po: e.matmul(
                    pO[po, :], Vt[:, kt, h * 64:(h + 1) * 64], p_[:, :], start=(kt == 0), stop=(kt == nk - 1)),
                     reads=[r_Vt, r_p], writes=[rO])
                S.op("pe", lambda e, p_=p_, kt=kt, nk=nk, pD=pD, po=po: e.matmul(
                    pD[po, :], C.ones_bf[:, 0:64], p_[:, :], start=(kt == 0), stop=(kt == nk - 1)),
                     reads=[C.r_ones, r_p], writes=[rDn])
            S.op("dve", lambda e, pD=pD, po=po: e.reciprocal(out=rD[po, :], in_=pD[po, :]), reads=[rDn], writes=[r_rD])
            S.op("dve", lambda e, pO=pO, po=po, h=h: e.tensor_tensor(out=ya[po, h // 2, :], in0=pO[po, :],
                                                                     in1=rD[po, :], op=ALU.mult),
                 reads=[rO, r_rD], writes=[r_ya])
        store_T(cx, y_o, tok0, ya, r_ya, 2)
    S.finish("sp")
    S.emit()
    return cx.nc


def rope_tables_host():
    half = 16
    inv = (10000.0 ** (-np.arange(half, dtype=np.float32) / half)).astype(np.float32)
    ang = np.arange(SEQ, dtype=np.float32)[None, :] * inv[:, None]
    c = np.cos(ang).astype(np.float32)
    s = np.sin(ang).astype(np.float32)
    return np.concatenate([c, c], 0), np.concatenate([-s, s], 0)


def mla_inputs(W, hg, cast):
    w_in = W["ev_w_in"][0]
    w_cq, w_ckv, w_krope = w_in[:, 0:384], w_in[:, 384:640], w_in[:, 640:672]
    zeros = lambda a, b: np.zeros((a, b), w_in.dtype)
    sw = np.concatenate([w_krope[:, 16:], w_krope[:, :16]], 1)
    w_kr = np.concatenate([zeros(D, 64), w_krope, zeros(D, 32)], 1)
    w_kr2 = np.concatenate([zeros(D, 64), sw, zeros(D, 32)], 1)
    wqu = W["mla_w_q_up"][0]
    wq, wq2 = [], []
    for h in range(4 * hg, 4 * hg + 4):
        nope = wqu[:, h * 96:h * 96 + 64]
        r = wqu[:, h * 96 + 64:h * 96 + 96]
        rs = np.concatenate([r[:, 16:], r[:, :16]], 1)
        z64 = np.zeros((384, 64), wqu.dtype)
        z32 = np.zeros((384, 32), wqu.dtype)
        wq.append(np.concatenate([nope, r, z32], 1))
        wq2.append(np.concatenate([z64, rs, z32], 1))
    wkv = W["mla_w_kv_up"][0]
    wk = np.concatenate([wkv[:, h * 128:h * 128 + 64] for h in range(4 * hg, 4 * hg + 4)], 1)
    wv = np.concatenate([wkv[:, h * 128 + 64:h * 128 + 128] for h in range(4 * hg, 4 * hg + 4)], 1)
    cos, sin = rope_tables_host()
    c = lambda a: np.ascontiguousarray(cast(a))
    return dict(w_cq=c(w_cq), w_ckv=c(w_ckv), w_kr=c(w_kr), w_kr2=c(w_kr2), wq=c(np.concatenate(wq, 1)),
                wq2=c(np.concatenate(wq2, 1)), wk=c(wk), wv=c(wv), q_norm=W["mla_q_norm"][0],
                kv_norm=W["mla_kv_norm"][0], cos=cos, sin=sin)


def build_gla(ntiles=NQT):
    cx = Ctx(bf_bank=True)
    S = cx.S
    xnT = cx.din("xnT", [D, SEQ], BF16)
    w_gq = cx.din("w_gq", [D, 128], BF16)
    w_gk = cx.din("w_gk", [D, 128], BF16)
    w_gv = cx.din("w_gv", [D, 256], BF16)
    w_gl = cx.din("w_gl", [D, 32], BF16)
    w_gr = cx.din("w_gr", [D, 256], BF16)
    wgu = cx.din("wgu", [32, 128], BF16)
    bg = cx.din("b_gate", [128], F32)
    onrm = cx.din("out_norm", [128], F32)
    y_o = cx.dout("yT", [256, SEQ], BF16)
    C = Consts(cx)
    wgq_t, r_wgq = load_resident(cx, w_gq, "wgq")
    wgk_t, r_wgk = load_resident(cx, w_gk, "wgk")
    wgv_t, r_wgv = load_resident(cx, w_gv, "wgv")
    wgl_t, r_wgl = load_resident(cx, w_gl, "wgl")
    wgr_t, r_wgr = load_resident(cx, w_gr, "wgr")
    wgu_t, r_wgu = cx.sb([32, 128], BF16, "wgu")
    S.dma("sp", lambda e: e.dma_start(out=wgu_t[:, :], in_=wgu[:, :]), writes=[r_wgu])
    bg_t, r_bg = load_vec_cols(cx, bg, 1, "bg")
    on_t, r_on = load_vec_cols(cx, onrm, 1, "onrm")
    mask01, r_m01 = cx.sb([128, TT], F32, "mask01")
    S.op("pool", lambda e: e.memset(mask01[:, :], 1.0), writes=[r_m01])
    for c in range(TT // 64):
        S.op("pool", lambda e, c=c: e.memset(mask01[:, c * 64:c * 64 + 1], 0.0), writes=[r_m01])
    tri, r_tri = cx.sb([64, 64], F32, "tri")
    S.op("pool", lambda e: e.memset(tri[:, :], 1.0), writes=[r_tri])
    S.op("pool", lambda e: e.affine_select(out=tri[:, :], in_=tri[:, :], pattern=[[1, 64]], compare_op=ALU.is_ge,
                                           fill=0.0, base=0, channel_multiplier=-1), reads=[r_tri], writes=[r_tri])
    S32, r_S32 = cx.sb([128, 128], F32, "S32")
    Sbf, r_Sbf = cx.sb([128, 128], BF16, "Sbf")
    S.op("pool", lambda e: e.memset(S32[:, :], 0.0), writes=[r_S32])
    S.op("pool", lambda e: e.memset(Sbf[:, :], 0.0), writes=[r_Sbf])
    xn = [cx.sb([128, KC, TT], BF16, "xn") for _ in range(2)]
    glT, r_glT = cx.sb([32, TT], BF16, "glT")
    lg, r_lg = cx.sb([128, TT], F32, "lg")
    bT, r_bT = cx.sb([128, TT], F32, "bT")
    eb, r_eb = cx.sb([128, TT], F32, "eb")
    enb, r_enb = cx.sb([128, TT], F32, "enb")
    dec, r_dec = cx.sb([128, TT // 64], F32, "dec")
    qe = [cx.sb([128, TT], BF16, "qe") for _ in range(2)]
    for h in range(2):
        S.op("pool", lambda e, h=h: e.memset(qe[h][0][:, :], 0.0), writes=[qe[h][1]])
    ke32, r_ke32 = cx.sb([128, TT], F32, "ke32")
    keT, r_keT = cx.sb([128, TT], BF16, "keT")
    kend, r_kend = cx.sb([128, TT], BF16, "kend")
    identb, r_identb = cx.sb([128, 128], BF16, "identb")
    S.op("act", lambda e: e.activation(out=identb[:, :], in_=C.ident[:, :], func=AF.Copy), reads=[C.r_ident],
         writes=[r_identb])
    pTb, rTb = cx.psum[7]
    tbi = 0
    sr = [cx.sb([128, TT], F32, "sr") for _ in range(2)]
    vtok = [cx.sb([64, 256], BF16, "vtok") for _ in range(3)]
    ktok = [cx.sb([64, 128], BF16, "ktok") for _ in range(3)]
    attb = [cx.sb([64, 64], BF16, "attb") for _ in range(4)]
    sq, r_sq = cx.sb([128, TT], BF16, "sq")
    rstd, r_rstd = cx.sb([128, TT], F32, "rstd")
    on32, r_on32 = cx.sb([128, TT], F32, "on32")
    yout, r_yout = cx.sb([128, 2, TT], BF16, "yout")
    vi = ki = ai = 0
    pOg = [cx.psum[0], cx.psum[1]]
    pKV, rKV = cx.psum[2]
    for t in range(ntiles):
        tok0 = t * TT
        x_t, r_x = xn[t % 2]
        load_T(cx, xnT, tok0, x_t, r_x, KC)
        b_, rb = cx.bank(3, 7)
        mm_group(S, b_[0:32, :], rb, [(wgl_t[:, k, :], x_t[:, k, :]) for k in range(KC)], reads=[r_wgl, r_x])
        S.op("act", lambda e, b_=b_: e.activation(out=glT[:, :], in_=b_[0:32, :], func=AF.Copy),
             reads=[rb], writes=[r_glT])
        bz, rbz = cx.bank(3, 7)
        S.op("pe", lambda e, bz=bz: e.matmul(bz[:, :], wgu_t[:, :], glT[:, :], start=True, stop=True),
             reads=[r_wgu, r_glT], writes=[rbz])
        S.op("act", lambda e, bz=bz: e.activation(out=lg[:, :], in_=bz[:, :], func=AF.Sigmoid, bias=bg_t[:, 0:1]),
             reads=[rbz, r_bg], writes=[r_lg])
        S.op("act", lambda e: e.activation(out=lg[:, :], in_=lg[:, :], func=AF.Ln), reads=[r_lg], writes=[r_lg])
        S.op("dve", lambda e: e.tensor_tensor_scan(out=bT[:, :], data0=mask01[:, :], data1=lg[:, :], initial=0.0,
                                                   op0=ALU.mult, op1=ALU.add),
             reads=[r_m01, r_lg], writes=[r_bT])
        S.op("act", lambda e: e.activation(out=eb[:, :], in_=bT[:, :], func=AF.Exp, scale=1.0 / 16),
             reads=[r_bT], writes=[r_eb])
        S.op("act", lambda e: e.activation(out=enb[:, :], in_=bT[:, :], func=AF.Exp, scale=-1.0 / 16),
             reads=[r_bT], writes=[r_enb])
        S.op("act", lambda e: e.activation(out=dec[:, :], in_=bT[:, 63::64], func=AF.Exp, scale=1.0 / 16),
             reads=[r_bT], writes=[r_dec])
        bq, rbq = cx.bank(3, 7)
        mm_group(S, bq[:, :], rbq, [(wgq_t[:, k, :], x_t[:, k, :]) for k in range(KC)], reads=[r_wgq, r_x])
        for h in range(2):
            hs = slice(h * 64, (h + 1) * 64)
            S.op("dve", lambda e, bq=bq, h=h, hs=hs: e.scalar_tensor_tensor(out=qe[h][0][hs, :], in0=bq[hs, :],
                                                                            scalar=0.125, in1=eb[hs, :],
                                                                            op0=ALU.mult, op1=ALU.mult),
                 reads=[rbq, r_eb], writes=[qe[h][1]])
        bk, rbk = cx.bank(3, 7)
        mm_group(S, bk[:, :], rbk, [(wgk_t[:, k, :], x_t[:, k, :]) for k in range(KC)], reads=[r_wgk, r_x])
        S.op("dve", lambda e, bk=bk: e.tensor_tensor(out=ke32[:, :], in0=bk[:, :], in1=enb[:, :], op=ALU.mult),
             reads=[rbk, r_enb], writes=[r_ke32])
        S.op("act", lambda e: e.activation(out=keT[:, :], in_=ke32[:, :], func=AF.Copy), reads=[r_ke32], writes=[r_keT])
        for c in range(TT // 64):
            S.op("dve", lambda e, c=c: e.tensor_scalar(out=kend[:, c * 64:(c + 1) * 64], in0=ke32[:, c * 64:(c + 1) * 64],
                                                       scalar1=dec[:, c:c + 1], scalar2=None, op0=ALU.mult),
                 reads=[r_ke32, r_dec], writes=[r_kend])
        for h in range(2):
            br, rbr = cx.bank(3, 7)
            mm_group(S, br[:, :], rbr, [(wgr_t[:, k, h * 128:(h + 1) * 128], x_t[:, k, :]) for k in range(KC)],
                     reads=[r_wgr, r_x])
            S.op("act", lambda e, br=br, h=h: e.activation(out=sr[h][0][:, :], in_=br[:, :], func=AF.Silu),
                 reads=[rbr], writes=[sr[h][1]])
        for c in range(TT // 64):
            cs_ = slice(c * 64, (c + 1) * 64)
            bv, rbv = cx.bank(3, 7)
            mm_group(S, bv[0:64, 0:256], rbv, [(x_t[:, k, cs_], wgv_t[:, k, :]) for k in range(KC)], reads=[r_wgv, r_x])
            v_, r_v = vtok[vi % 3]
            vi += 1
            S.op("act", lambda e, bv=bv, v_=v_: e.activation(out=v_[:, :], in_=bv[0:64, 0:256], func=AF.Copy),
                 reads=[rbv], writes=[r_v])
            tb0 = (tbi % 8) * 128
            tbi += 1
            S.op("pe", lambda e, cs_=cs_, tb0=tb0: e.transpose(out=pTb[0:64, tb0:tb0 + 128], in_=kend[:, cs_],
                                                                identity=identb[:, :]),
                 reads=[r_kend, r_identb], writes=[rTb])
            k_, r_k = ktok[ki % 3]
            ki += 1
            S.op("act", lambda e, k_=k_, tb0=tb0: e.activation(out=k_[:, :], in_=pTb[0:64, tb0:tb0 + 128], func=AF.Copy),
                 reads=[rTb], writes=[r_k])
            for h in range(2):
                hs = slice(h * 64, (h + 1) * 64)
                q_, r_q = qe[h]
                ba, rba = cx.bank(3, 7)
                S.op("pe", lambda e, ba=ba, q_=q_, cs_=cs_: e.matmul(ba[0:64, 0:64], keT[:, cs_], q_[:, cs_],
                                                                     start=True, stop=True),
                     reads=[r_keT, r_q], writes=[rba])
                a_, r_a = attb[ai % 4]
                ai += 1
                S.op("dve", lambda e, ba=ba, a_=a_: e.tensor_tensor(out=a_[:, :], in0=ba[0:64, 0:64], in1=tri[:, :],
                                                                    op=ALU.mult),
                     reads=[rba, r_tri], writes=[r_a])
                po, rpo = pOg[h]
                pe_group(S, [
                    (lambda e, po=po, v_=v_, a_=a_, h=h, cs_=cs_: e.matmul(po[:, cs_], v_[:, h * 128:(h + 1) * 128],
                                                                           a_[:, :], start=True, stop=False)),
                    (lambda e, po=po, q_=q_, cs_=cs_: e.matmul(po[:, cs_], Sbf[:, :], q_[:, cs_],
                                                               start=False, stop=True)),
                ], reads=[r_v, r_a, r_Sbf, r_q], out_res=rpo)
                S.op("pe", lambda e, k_=k_, v_=v_, hs=hs, h=h: e.matmul(pKV[hs, 0:128], k_[:, hs],
                                                                        v_[:, h * 128:(h + 1) * 128],
                                                                        start=True, stop=True),
                     reads=[r_k, r_v], writes=[rKV])
            S.op("dve", lambda e, c=c: e.scalar_tensor_tensor(out=S32[:, :], in0=S32[:, :], scalar=dec[:, c:c + 1],
                                                              in1=pKV[:, 0:128], op0=ALU.mult, op1=ALU.add),
                 reads=[r_S32, r_dec, rKV], writes=[r_S32])
            S.op("act", lambda e: e.activation(out=Sbf[:, :], in_=S32[:, :], func=AF.Copy),
                 reads=[r_S32], writes=[r_Sbf])
        for h in range(2):
            po, rpo = pOg[h]
            S.op("act", lambda e, po=po: e.activation(out=sq[:, :], in_=po[:, :], func=AF.Square),
                 reads=[rpo], writes=[r_sq])
            bs, rbs = cx.bank(3, 7)
            S.op("pe", lambda e, bs=bs: e.matmul(bs[:, :], C.ones_bf[:, :], sq[:, :], start=True, stop=True),
                 reads=[C.r_ones, r_sq], writes=[rbs])
            S.op("act", lambda e, bs=bs: e.activation(out=rstd[:, :], in_=bs[:, :], func=AF.Sqrt, scale=1.0 / 128,
                                                      bias=EPS), reads=[rbs], writes=[r_rstd])
            S.op("dve", lambda e: e.reciprocal(out=rstd[:, :], in_=rstd[:, :]), reads=[r_rstd], writes=[r_rstd])
            S.op("dve", lambda e, po=po: e.scalar_tensor_tensor(out=on32[:, :], in0=po[:, :], scalar=on_t[:, 0:1],
                                                                in1=rstd[:, :], op0=ALU.mult, op1=ALU.mult),
                 reads=[rpo, r_on, r_rstd], writes=[r_on32])
            S.op("dve", lambda e, h=h: e.tensor_tensor(out=yout[:, h, :], in0=on32[:, :], in1=sr[h][0][:, :],
                                                       op=ALU.mult),
                 reads=[r_on32, sr[h][1]], writes=[r_yout])
        store_T(cx, y_o, tok0, yout, r_yout, 2)
    S.finish("sp")
    S.emit()
    return cx.nc


def gla_inputs(W, hg, cast):
    w_in = W["ev_w_in"][0]
    o = 672
    gq, gk, gv = w_in[:, o:o + 256], w_in[:, o + 256:o + 512], w_in[:, o + 512:o + 1024]
    gl, gr = w_in[:, o + 1024:o + 1040], w_in[:, o + 1040:o + 1552]
    hs = slice(128 * hg, 128 * hg + 128)
    vs = slice(256 * hg, 256 * hg + 256)
    c = lambda a: np.ascontiguousarray(cast(a))
    gl = np.concatenate([gl, np.zeros_like(gl)], 1)
    gu = W["gla_w_gate_up"][0][:, hs]
    gu = np.concatenate([gu, np.zeros_like(gu)], 0)
    return dict(w_gq=c(gq[:, hs]), w_gk=c(gk[:, hs]), w_gv=c(gv[:, vs]), w_gl=c(gl), w_gr=c(gr[:, vs]),
                wgu=c(gu), b_gate=np.ascontiguousarray(W["gla_b_gate"][0][hs]),
                out_norm=W["gla_out_norm"][0])


def gelu_tanh_mul(cx, src_ap, r_src, other_ap, r_other, out_ap, r_out, tmp, P=slice(0, 128)):
    S = cx.S
    t1, r1, t2, r2 = tmp
    S.op("act", lambda e: e.activation(out=t1[P, :], in_=src_ap, func=AF.Square), reads=[r_src], writes=[r1])
    S.op("dve", lambda e: e.tensor_scalar(out=t1[P, :], in0=t1[P, :], scalar1=0.044715, scalar2=1.0,
                                          op0=ALU.mult, op1=ALU.add), reads=[r1], writes=[r1])
    S.op("dve", lambda e: e.tensor_tensor(out=t1[P, :], in0=src_ap, in1=t1[P, :], op=ALU.mult),
         reads=[r_src, r1], writes=[r1])
    S.op("act", lambda e: e.activation(out=t1[P, :], in_=t1[P, :], func=AF.Sigmoid, scale=1.5957691216),
         reads=[r1], writes=[r1])
    if other_ap is None:
        S.op("dve", lambda e: e.tensor_tensor(out=out_ap, in0=src_ap, in1=t1[P, :], op=ALU.mult),
             reads=[r_src, r1], writes=[r_out])
    else:
        S.op("dve", lambda e: e.tensor_tensor(out=t2[P, :], in0=src_ap, in1=t1[P, :], op=ALU.mult),
             reads=[r_src, r1], writes=[r2])
        S.op("dve", lambda e: e.tensor_tensor(out=out_ap, in0=t2[P, :], in1=other_ap, op=ALU.mult),
             reads=[r2, r_other], writes=[r_out])


def build_rglru(ntiles=NQT):
    cx = Ctx()
    S = cx.S
    xnT = cx.din("xnT", [D, SEQ], BF16)
    w_g = cx.din("w_g", [D, 256], BF16)
    w_r = cx.din("w_r", [D, 256], BF16)
    wa = cx.din("wa_bd", [256, 128], BF16)
    wi = cx.din("wi_bd", [256, 128], BF16)
    cw = cx.din("conv_w", [4, 256], F32)
    cb = cx.din("conv_b", [256], F32)
    ba_ = cx.din("b_a", [256], F32)
    bi_ = cx.din("b_i", [256], F32)
    lam = cx.din("lam", [256], F32)
    y_o = cx.dout("yT", [256, SEQ], BF16)
    C = Consts(cx)
    wg_t, r_wg = load_resident(cx, w_g, "wg")
    wr_t, r_wr = load_resident(cx, w_r, "wr")
    wa_t, r_wa = load_resident(cx, wa, "wa")
    wi_t, r_wi = load_resident(cx, wi, "wi")
    cwk = [load_vec_cols(cx, cw[k, :], 2, f"cw{k}") for k in range(4)]
    cb_t, r_cb = load_vec_cols(cx, cb, 2, "cb")
    ba_t, r_ba = load_vec_cols(cx, ba_, 2, "ba")
    bi_t, r_bi = load_vec_cols(cx, bi_, 2, "bi")
    lam_t, r_lam = load_vec_cols(cx, lam, 2, "lam")
    cA, r_cA = cx.sb([128, 2], F32, "cA")
    S.op("act", lambda e: e.activation(out=cA[:, :], in_=lam_t[:, :], func=AF.Exp, scale=-1.0), reads=[r_lam], writes=[r_cA])
    S.op("act", lambda e: e.activation(out=cA[:, :], in_=cA[:, :], func=AF.Ln, bias=1.0), reads=[r_cA], writes=[r_cA])
    S.op("dve", lambda e: e.tensor_scalar(out=cA[:, :], in0=cA[:, :], scalar1=-8.0, scalar2=None, op0=ALU.mult),
         reads=[r_cA], writes=[r_cA])
    xn = [cx.sb([128, KC, TT], BF16, "xn") for _ in range(2)]
    xr = [cx.sb([128, 3 + TT], F32, "xr") for _ in range(2)]
    for ct in range(2):
        S.op("pool", lambda e, ct=ct: e.memset(xr[ct][0][:, 0:3], 0.0), writes=[xr[ct][1]])
    carry, r_carry = cx.sb([128, 4], F32, "carry")
    S.op("pool", lambda e: e.memset(carry[:, :], 0.0), writes=[r_carry])
    xc, r_xc = cx.sb([128, TT], F32, "xc")
    xcb, r_xcb = cx.sb([128, TT], BF16, "xcb")
    rg, r_rg = cx.sb([128, TT], F32, "rg")
    ig, r_ig = cx.sb([128, TT], F32, "ig")
    av, r_av = cx.sb([128, TT], F32, "av")
    om, r_om = cx.sb([128, TT], F32, "om")
    bx, r_bx = cx.sb([128, TT], F32, "bx")
    hh, r_hh = cx.sb([128, TT], F32, "hh")
    t1, r_t1 = cx.sb([128, TT], F32, "t1")
    t2, r_t2 = cx.sb([128, TT], F32, "t2")
    yout, r_yout = cx.sb([128, 2, TT], BF16, "yout")
    for t in range(ntiles):
        tok0 = t * TT
        x_t, r_x = xn[t % 2]
        load_T(cx, xnT, tok0, x_t, r_x, KC)
        for ct in range(2):
            xr_, r_xr = xr[ct]
            cs_ = slice(ct * 128, (ct + 1) * 128)
            bR, rbR = cx.bank(0, 8)
            mm_group(S, bR[:, :], rbR, [(wr_t[:, k, cs_], x_t[:, k, :]) for k in range(KC)], reads=[r_wr, r_x])
            S.op("act", lambda e, xr_=xr_, bR=bR: e.activation(out=xr_[:, 3:3 + TT], in_=bR[:, :], func=AF.Copy),
                 reads=[rbR], writes=[r_xr])
            S.op("act", lambda e, xr_=xr_, ct=ct: e.activation(out=xc[:, :], in_=xr_[:, 3:3 + TT], func=AF.Identity,
                                                               scale=cwk[3][0][:, ct:ct + 1], bias=cb_t[:, ct:ct + 1]),
                 reads=[r_xr, cwk[3][1], r_cb], writes=[r_xc])
            for k in range(3):
                S.op("dve", lambda e, xr_=xr_, ct=ct, k=k: e.scalar_tensor_tensor(
                    out=xc[:, :], in0=xr_[:, k:k + TT], scalar=cwk[k][0][:, ct:ct + 1], in1=xc[:, :],
                    op0=ALU.mult, op1=ALU.add), reads=[r_xr, cwk[k][1], r_xc], writes=[r_xc])
            S.op("act", lambda e: e.activation(out=xcb[:, :], in_=xc[:, :], func=AF.Copy), reads=[r_xc], writes=[r_xcb])
            S.op("act", lambda e, xr_=xr_: e.activation(out=xr_[:, 0:3], in_=xr_[:, TT:TT + 3], func=AF.Copy),
                 reads=[r_xr, r_xc], writes=[r_xr])
            bA, rbA = cx.bank(0, 8)
            S.op("pe", lambda e, bA=bA, ct=ct: e.matmul(bA[:, :], wa_t[:, ct, :], xcb[:, :], start=True, stop=True),
                 reads=[r_wa, r_xcb], writes=[rbA])
            bI, rbI = cx.bank(0, 8)
            S.op("pe", lambda e, bI=bI, ct=ct: e.matmul(bI[:, :], wi_t[:, ct, :], xcb[:, :], start=True, stop=True),
                 reads=[r_wi, r_xcb], writes=[rbI])
            S.op("act", lambda e, bA=bA, ct=ct: e.activation(out=rg[:, :], in_=bA[:, :], func=AF.Sigmoid,
                                                             bias=ba_t[:, ct:ct + 1]), reads=[rbA, r_ba], writes=[r_rg])
            S.op("act", lambda e, bI=bI, ct=ct: e.activation(out=ig[:, :], in_=bI[:, :], func=AF.Sigmoid,
                                                             bias=bi_t[:, ct:ct + 1]), reads=[rbI, r_bi], writes=[r_ig])
            S.op("act", lambda e, ct=ct: e.activation(out=av[:, :], in_=rg[:, :], func=AF.Exp, scale=cA[:, ct:ct + 1]),
                 reads=[r_rg, r_cA], writes=[r_av])
            S.op("act", lambda e: e.activation(out=om[:, :], in_=av[:, :], func=AF.Square), reads=[r_av], writes=[r_om])
            S.op("act", lambda e: e.activation(out=om[:, :], in_=om[:, :], func=AF.Sqrt, scale=-1.0, bias=1.0),
                 reads=[r_om], writes=[r_om])
            S.op("dve", lambda e: e.tensor_tensor(out=bx[:, :], in0=ig[:, :], in1=xc[:, :], op=ALU.mult),
                 reads=[r_ig, r_xc], writes=[r_bx])
            S.op("dve", lambda e: e.tensor_tensor(out=bx[:, :], in0=bx[:, :], in1=om[:, :], op=ALU.mult),
                 reads=[r_bx, r_om], writes=[r_bx])
            S.op("act", lambda e, ct=ct: e.activation(out=carry[:, 2:3], in_=carry[:, ct:ct + 1], func=AF.Identity,
                                                      scale=av[:, 0:1], bias=bx[:, 0:1]),
                 reads=[r_carry, r_av, r_bx], writes=[r_carry])
            S.op("act", lambda e: e.activation(out=bx[:, 0:1], in_=carry[:, 2:3], func=AF.Copy),
                 reads=[r_carry], writes=[r_bx])
            S.op("dve", lambda e: e.tensor_tensor_scan(out=hh[:, :], data0=av[:, :], data1=bx[:, :],
                                                       initial=0.0, op0=ALU.mult, op1=ALU.add),
                 reads=[r_av, r_bx], writes=[r_hh])
            S.op("act", lambda e, ct=ct: e.activation(out=carry[:, ct:ct + 1], in_=hh[:, TT - 1:TT], func=AF.Copy),
                 reads=[r_hh], writes=[r_carry])
            bG, rbG = cx.bank(0, 8)
            mm_group(S, bG[:, :], rbG, [(wg_t[:, k, cs_], x_t[:, k, :]) for k in range(KC)], reads=[r_wg, r_x])
            gelu_tanh_mul(cx, bG[:, :], rbG, hh[:, :], r_hh, yout[:, ct, :], r_yout, (t1, r_t1, t2, r_t2))
        store_T(cx, y_o, tok0, yout, r_yout, 2)
    S.finish("sp")
    S.emit()
    return cx.nc


def block_diag_pairs(w):
    nb = w.shape[0]
    out = np.zeros((nb // 2, 128, 128), w.dtype)
    for i in range(nb):
        j, o = divmod(i, 2)
        out[j, o * 64:(o + 1) * 64, o * 64:(o + 1) * 64] = w[i]
    return out.reshape(nb // 2 * 128, 128)


def rglru_inputs(W, half, cast):
    w_in = W["od_w_in"][0]
    cs_ = slice(256 * half, 256 * half + 256)
    bs_ = slice(4 * half, 4 * half + 4)
    c = lambda a: np.ascontiguousarray(cast(a))
    return dict(w_g=c(w_in[:, 0:512][:, cs_]), w_r=c(w_in[:, 512:1024][:, cs_]),
                wa_bd=c(block_diag_pairs(W["rg_w_a"][0][bs_])), wi_bd=c(block_diag_pairs(W["rg_w_i"][0][bs_])),
                conv_w=np.ascontiguousarray(W["rg_conv_w"][0][:, cs_]), conv_b=np.ascontiguousarray(W["rg_conv_b"][0][cs_]),
                b_a=np.ascontiguousarray(W["rg_b_a"][0][cs_]), b_i=np.ascontiguousarray(W["rg_b_i"][0][cs_]),
                lam=np.ascontiguousarray(W["rg_lambda"][0][cs_]))


NST = 8
HALF_PI = 1.5707963267948966


def build_s5(ntiles=NQT):
    cx = Ctx()
    S = cx.S
    xnT = cx.din("xnT", [D, SEQ], BF16)
    w_u = cx.din("w_u", [D, 256], BF16)
    lr_d = cx.din("lr", [128, NST], F32)
    li_d = cx.din("li", [128, NST], F32)
    ldt_d = cx.din("ldt", [128, NST], F32)
    bre_d = cx.din("b_re", [128, NST * 16], F32)
    bim_d = cx.din("b_im", [128, NST * 16], F32)
    lcre_d = cx.din("lc_re", [NST * 128, 128], F32)
    lcim_d = cx.din("lc_im", [NST * 128, 128], F32)
    d_d = cx.din("d", [256], F32)
    y_o = cx.dout("yT", [256, SEQ], BF16)
    C = Consts(cx)
    wu_t, r_wu = load_resident(cx, w_u, "wu")
    d_t, r_d = load_vec_cols(cx, d_d, 2, "d")

    def small(name, src=None, n=NST):
        t, r = cx.sb([128, n], F32, name)
        if src is not None:
            S.dma("sp", lambda e: e.dma_start(out=t[:, :], in_=src[:, :]), writes=[r])
        return t, r

    lr, r_lr = small("lr", lr_d)
    li, r_li = small("li", li_d)
    dt, r_dt = small("dt", ldt_d)
    bre, r_bre = small("bre", bre_d, NST * 16)
    bim, r_bim = small("bim", bim_d, NST * 16)

    def act(out, in_, func, reads, writes, **kw):
        S.op("act", lambda e: e.activation(out=out, in_=in_, func=func, **kw), reads=reads, writes=writes)

    def tt(out, a, b, op, reads, writes, eng="dve"):
        S.op(eng, lambda e: e.tensor_tensor(out=out, in0=a, in1=b, op=op), reads=reads, writes=writes)

    def ts(out, a, s1, op0, reads, writes, s2=None, op1=None, eng="dve"):
        if op1 is None:
            S.op(eng, lambda e: e.tensor_scalar(out=out, in0=a, scalar1=s1, scalar2=None, op0=op0), reads=reads,
                 writes=writes)
        else:
            S.op(eng, lambda e: e.tensor_scalar(out=out, in0=a, scalar1=s1, scalar2=s2, op0=op0, op1=op1),
                 reads=reads, writes=writes)

    def stt(out, a, s, b, op0, op1, reads, writes):
        S.op("dve", lambda e: e.scalar_tensor_tensor(out=out, in0=a, scalar=s, in1=b, op0=op0, op1=op1),
             reads=reads, writes=writes)

    act(dt[:, :], dt[:, :], AF.Exp, [r_dt], [r_dt])
    lrd, r_lrd = small("lrd")
    th, r_th = small("th")
    tt(lrd[:, :], lr[:, :], dt[:, :], ALU.mult, [r_lr, r_dt], [r_lrd])
    tt(th[:, :], li[:, :], dt[:, :], ALU.mult, [r_li, r_dt], [r_th])
    mag, r_mag = small("mag")
    act(mag[:, :], lrd[:, :], AF.Exp, [r_lrd], [r_mag])
    cc, r_cc = small("cc")
    ss, r_ss = small("ss")
    ta, r_ta = small("ta")
    tb, r_tb = small("tb")
    hp, r_hp = small("hp", n=1)
    S.op("pool", lambda e: e.memset(hp[:, :], HALF_PI), writes=[r_hp])
    act(ss[:, :], th[:, :], AF.Sin, [r_th], [r_ss], scale=1.0 / 64)
    act(cc[:, :], th[:, :], AF.Sin, [r_th, r_hp], [r_cc], scale=1.0 / 64, bias=hp[:, 0:1])

    def cdouble(c_, r_c, s_, r_s):
        tt(ta[:, :], c_, c_, ALU.mult, [r_c], [r_ta])
        tt(tb[:, :], s_, s_, ALU.mult, [r_s], [r_tb])
        stt(s_, c_, 2.0, s_, ALU.mult, ALU.mult, [r_c, r_s], [r_s])
        tt(c_, ta[:, :], tb[:, :], ALU.subtract, [r_ta, r_tb], [r_c])

    for _ in range(6):
        cdouble(cc[:, :], r_cc, ss[:, :], r_ss)
    NP = 10
    pw_r, r_pwr = small("pwr", n=NP * NST)
    pw_i, r_pwi = small("pwi", n=NP * NST)
    pw_ni, r_pwni = small("pwni", n=NP * NST)
    act(pw_r[:, 0:NST], cc[:, :], AF.Copy, [r_cc], [r_pwr])
    act(pw_i[:, 0:NST], ss[:, :], AF.Copy, [r_ss], [r_pwi])
    for k in range(1, NP):
        a0, a1, b0, b1 = (k - 1) * NST, k * NST, k * NST, (k + 1) * NST
        tt(ta[:, :], pw_r[:, a0:a1], pw_r[:, a0:a1], ALU.mult, [r_pwr], [r_ta])
        tt(tb[:, :], pw_i[:, a0:a1], pw_i[:, a0:a1], ALU.mult, [r_pwi], [r_tb])
        stt(pw_i[:, b0:b1], pw_r[:, a0:a1], 2.0, pw_i[:, a0:a1], ALU.mult, ALU.mult, [r_pwr, r_pwi], [r_pwi])
        tt(pw_r[:, b0:b1], ta[:, :], tb[:, :], ALU.subtract, [r_ta, r_tb], [r_pwr])
    ts(pw_ni[:, :], pw_i[:, :], -1.0, ALU.mult, [r_pwi], [r_pwni])
    ar, r_ar = small("ar")
    ai, r_ai = small("ai")
    tt(ar[:, :], mag[:, :], cc[:, :], ALU.mult, [r_mag, r_cc], [r_ar])
    tt(ai[:, :], mag[:, :], ss[:, :], ALU.mult, [r_mag, r_ss], [r_ai])
    ts(ar[:, :], ar[:, :], -1.0, ALU.add, [r_ar], [r_ar])
    den, r_den = small("den")
    tt(ta[:, :], lr[:, :], lr[:, :], ALU.mult, [r_lr], [r_ta])
    tt(tb[:, :], li[:, :], li[:, :], ALU.mult, [r_li], [r_tb])
    tt(den[:, :], ta[:, :], tb[:, :], ALU.add, [r_ta, r_tb], [r_den])
    S.op("dve", lambda e: e.reciprocal(out=den[:, :], in_=den[:, :]), reads=[r_den], writes=[r_den])
    cre, r_cre = small("cre")
    cim, r_cim = small("cim")
    ncim, r_ncim = small("ncim")
    tt(ta[:, :], ar[:, :], lr[:, :], ALU.mult, [r_ar, r_lr], [r_ta])
    tt(tb[:, :], ai[:, :], li[:, :], ALU.mult, [r_ai, r_li], [r_tb])
    tt(cre[:, :], ta[:, :], tb[:, :], ALU.add, [r_ta, r_tb], [r_cre])
    tt(cre[:, :], cre[:, :], den[:, :], ALU.mult, [r_cre, r_den], [r_cre])
    tt(ta[:, :], ai[:, :], lr[:, :], ALU.mult, [r_ai, r_lr], [r_ta])
    tt(tb[:, :], ar[:, :], li[:, :], ALU.mult, [r_ar, r_li], [r_tb])
    tt(cim[:, :], ta[:, :], tb[:, :], ALU.subtract, [r_ta, r_tb], [r_cim])
    tt(cim[:, :], cim[:, :], den[:, :], ALU.mult, [r_cim, r_den], [r_cim])
    ts(ncim[:, :], cim[:, :], -1.0, ALU.mult, [r_cim], [r_ncim])
    LB = []
    LC = []
    BBr, r_BBr = cx.sb([128, 128], F32, "BBr")
    BBi, r_BBi = cx.sb([128, 128], F32, "BBi")
    lc32, r_lc32 = cx.sb([128, 128], F32, "lc32")
    for st in range(NST):
        S.op("pool", lambda e: e.memset(BBr[:, :], 0.0), writes=[r_BBr])
        S.op("pool", lambda e: e.memset(BBi[:, :], 0.0), writes=[r_BBi])
        for j in range(2):
            gl = (2 * st + j) % 8
            P_ = slice(j * 64, (j + 1) * 64)
            cols = slice(gl * 16, (gl + 1) * 16)
            br_ = bre[P_, st * 16:(st + 1) * 16]
            bi_ = bim[P_, st * 16:(st + 1) * 16]
            ts(BBr[P_, cols], br_, cre[P_, st:st + 1], ALU.mult, [r_bre, r_cre], [r_BBr])
            stt(BBr[P_, cols], bi_, ncim[P_, st:st + 1], BBr[P_, cols], ALU.mult, ALU.add, [r_bim, r_ncim, r_BBr], [r_BBr])
            ts(BBi[P_, cols], bi_, cre[P_, st:st + 1], ALU.mult, [r_bim, r_cre], [r_BBi])
            stt(BBi[P_, cols], br_, cim[P_, st:st + 1], BBi[P_, cols], ALU.mult, ALU.add, [r_bre, r_cim, r_BBi], [r_BBi])
        pair = []
        for src, r_src in ((BBr, r_BBr), (BBi, r_BBi)):
            bank, rb = cx.bank(0, 8)
            S.op("pe", lambda e, bank=bank, src=src: e.transpose(out=bank[:, 0:128], in_=src[:, :], identity=C.ident[:, :]),
                 reads=[r_src, C.r_ident], writes=[rb])
            lt, r_lt = cx.sb([128, 128], BF16, "LB")
            act(lt[:, :], bank[:, 0:128], AF.Copy, [rb], [r_lt])
            pair.append((lt, r_lt))
        LB.append(pair)
        pair = []
        for src_d, sc in ((lcre_d, 1.0), (lcim_d, -1.0)):
            S.dma("sp", lambda e, src_d=src_d, st=st: e.dma_start(out=lc32[:, :], in_=src_d[st * 128:(st + 1) * 128, :]),
                  writes=[r_lc32])
            lt, r_lt = cx.sb([128, 128], BF16, "LC")
            act(lt[:, :], lc32[:, :], AF.Copy, [r_lc32], [r_lt], scale=sc)
            pair.append((lt, r_lt))
        LC.append(pair)
    ones32, r_ones32 = cx.sb([128, TT], F32, "ones32")
    S.op("pool", lambda e: e.memset(ones32[:, :], 1.0), writes=[r_ones32])
    cosT, sinT, rtab = [], [], []
    for st in range(NST):
        ct_, r_ct = cx.sb([128, TT], F32, "cosT")
        st_, r_st = cx.sb([128, TT], F32, "sinT")
        rt_, r_rt = cx.sb([128, TT], F32, "rtab")
        S.op("pool", lambda e, ct_=ct_: e.memset(ct_[:, 0:1], 1.0), writes=[r_ct])
        S.op("pool", lambda e, st_=st_: e.memset(st_[:, 0:1], 0.0), writes=[r_st])
        for k in range(9):
            n = 1 << k
            pr = pw_r[:, k * NST + st:k * NST + st + 1]
            pi = pw_i[:, k * NST + st:k * NST + st + 1]
            npi = pw_ni[:, k * NST + st:k * NST + st + 1]
            ts(ct_[:, n:2 * n], ct_[:, 0:n], pr, ALU.mult, [r_ct, r_pwr], [r_ct])
            stt(ct_[:, n:2 * n], st_[:, 0:n], npi, ct_[:, n:2 * n], ALU.mult, ALU.add, [r_st, r_pwni, r_ct], [r_ct])
            ts(st_[:, n:2 * n], ct_[:, 0:n], pi, ALU.mult, [r_ct, r_pwi], [r_st])
            stt(st_[:, n:2 * n], st_[:, 0:n], pr, st_[:, n:2 * n], ALU.mult, ALU.add, [r_st, r_pwr], [r_st])
        ts(rt_[:, :], ones32[:, :], mag[:, st:st + 1], ALU.mult, [r_ones32, r_mag], [r_rt])
        cosT.append((ct_, r_ct))
        sinT.append((st_, r_st))
        rtab.append((rt_, r_rt))
    PTr = pw_r[:, 9 * NST:10 * NST]
    PTi = pw_i[:, 9 * NST:10 * NST]
    car_r, r_car_r = small("car_r")
    car_i, r_car_i = small("car_i")
    S.op("pool", lambda e: e.memset(car_r[:, :], 0.0), writes=[r_car_r])
    S.op("pool", lambda e: e.memset(car_i[:, :], 0.0), writes=[r_car_i])
    cz, r_cz = small("cz", n=4)
    ini, r_ini = small("ini", n=2)
    xn = [cx.sb([128, KC, TT], BF16, "xn") for _ in range(2)]
    u32s = [cx.sb([128, 2, TT], F32, "u32") for _ in range(2)]
    ubs = [cx.sb([128, 2, TT], BF16, "ub") for _ in range(2)]
    ypre, r_ypre = cx.sb([128, TT], F32, "ypre")
    m = [cx.sb([128, TT], F32, f"m{i}") for i in range(4)]
    wre, r_wre = cx.sb([128, TT], F32, "wre")
    wim, r_wim = cx.sb([128, TT], F32, "wim")
    zre, r_zre = cx.sb([128, TT], F32, "zre")
    zim, r_zim = cx.sb([128, TT], F32, "zim")
    xrb, r_xrb = cx.sb([128, NST, TT], BF16, "xrb")
    xib, r_xib = cx.sb([128, NST, TT], BF16, "xib")
    yss, r_yss = cx.sb([128, TT], F32, "yss")
    t1, r_t1 = cx.sb([128, TT], F32, "t1")
    t2, r_t2 = cx.sb([128, TT], F32, "t2")
    yout, r_yout = cx.sb([128, 2, TT], BF16, "yout")
    for t in range(ntiles):
        tok0 = t * TT
        x_t, r_x = xn[t % 2]
        u32, r_u32 = u32s[t % 2]
        ub, r_ub = ubs[t % 2]
        load_T(cx, xnT, tok0, x_t, r_x, KC)
        for ct in range(2):
            bU, rbU = cx.bank(0, 8)
            mm_group(S, bU[:, :], rbU, [(wu_t[:, k, ct * 128:(ct + 1) * 128], x_t[:, k, :]) for k in range(KC)],
                     reads=[r_wu, r_x])
            act(u32[:, ct, :], bU[:, :], AF.Copy, [rbU], [r_u32])
            act(ub[:, ct, :], bU[:, :], AF.Copy, [rbU], [r_ub])
        for st in range(NST):
            ct = st // 4
            ct_, r_ct = cosT[st]
            st_, r_st = sinT[st]
            rt_, r_rt = rtab[st]
            pBr, rBr = cx.bank(0, 8)
            S.op("pe", lambda e, pBr=pBr, st=st, ct=ct, ub=ub: e.matmul(pBr[:, :], LB[st][0][0][:, :], ub[:, ct, :],
                                                                 start=True, stop=True),
                 reads=[LB[st][0][1], r_ub], writes=[rBr])
            pBi, rBi = cx.bank(0, 8)
            S.op("pe", lambda e, pBi=pBi, st=st, ct=ct, ub=ub: e.matmul(pBi[:, :], LB[st][1][0][:, :], ub[:, ct, :],
                                                                 start=True, stop=True),
                 reads=[LB[st][1][1], r_ub], writes=[rBi])
            tt(m[0][0][:, :], pBr[:, :], ct_[:, :], ALU.mult, [rBr, r_ct], [m[0][1]])
            tt(m[1][0][:, :], pBi[:, :], st_[:, :], ALU.mult, [rBi, r_st], [m[1][1]])
            tt(m[2][0][:, :], pBi[:, :], ct_[:, :], ALU.mult, [rBi, r_ct], [m[2][1]])
            tt(m[3][0][:, :], pBr[:, :], st_[:, :], ALU.mult, [rBr, r_st], [m[3][1]])
            tt(wre[:, :], m[0][0][:, :], m[1][0][:, :], ALU.add, [m[0][1], m[1][1]], [r_wre], eng="dve")
            tt(wim[:, :], m[2][0][:, :], m[3][0][:, :], ALU.subtract, [m[2][1], m[3][1]], [r_wim], eng="dve")
            act(cz[:, 2:3], car_r[:, st:st + 1], AF.Identity, [r_car_r, r_mag, r_wre], [r_cz],
                scale=mag[:, st:st + 1], bias=wre[:, 0:1])
            act(cz[:, 3:4], car_i[:, st:st + 1], AF.Identity, [r_car_i, r_mag, r_wim], [r_cz],
                scale=mag[:, st:st + 1], bias=wim[:, 0:1])
            act(wre[:, 0:1], cz[:, 2:3], AF.Copy, [r_cz], [r_wre])
            act(wim[:, 0:1], cz[:, 3:4], AF.Copy, [r_cz], [r_wim])
            S.op("dve", lambda e, rt_=rt_, st=st: e.tensor_tensor_scan(out=zre[:, :], data0=rt_[:, :], data1=wre[:, :],
                                                                       initial=0.0, op0=ALU.mult,
                                                                       op1=ALU.add),
                 reads=[r_rt, r_wre], writes=[r_zre])
            S.op("dve", lambda e, rt_=rt_, st=st: e.tensor_tensor_scan(out=zim[:, :], data0=rt_[:, :], data1=wim[:, :],
                                                                       initial=0.0, op0=ALU.mult,
                                                                       op1=ALU.add),
                 reads=[r_rt, r_wim], writes=[r_zim])
            L = slice(TT - 1, TT)
            s1 = slice(st, st + 1)
            nPTi = pw_ni[:, 9 * NST:10 * NST]
            act(cz[:, 0:1], zim[:, L], AF.Identity, [r_zim, r_pwni], [r_cz], scale=nPTi[:, s1])
            act(cz[:, 1:2], zim[:, L], AF.Identity, [r_zim, r_pwr], [r_cz], scale=PTr[:, s1])
            act(car_r[:, s1], zre[:, L], AF.Identity, [r_zre, r_pwr, r_cz], [r_car_r], scale=PTr[:, s1], bias=cz[:, 0:1])
            act(car_i[:, s1], zre[:, L], AF.Identity, [r_zre, r_pwi, r_cz], [r_car_i], scale=PTi[:, s1], bias=cz[:, 1:2])
            tt(m[0][0][:, :], zre[:, :], ct_[:, :], ALU.mult, [r_zre, r_ct], [m[0][1]], eng="dve")
            tt(m[1][0][:, :], zim[:, :], st_[:, :], ALU.mult, [r_zim, r_st], [m[1][1]], eng="dve")
            tt(m[2][0][:, :], zre[:, :], st_[:, :], ALU.mult, [r_zre, r_st], [m[2][1]], eng="dve")
            tt(m[3][0][:, :], zim[:, :], ct_[:, :], ALU.mult, [r_zim, r_ct], [m[3][1]], eng="dve")
            tt(xrb[:, st, :], m[0][0][:, :], m[1][0][:, :], ALU.subtract, [m[0][1], m[1][1]], [r_xrb], eng="dve")
            tt(xib[:, st, :], m[2][0][:, :], m[3][0][:, :], ALU.add, [m[2][1], m[3][1]], [r_xib], eng="dve")
        for ct in range(2):
            pY, rY = cx.bank(0, 8)
            pairs = []
            rd = []
            for st in range(ct * 4, ct * 4 + 4):
                pairs.append((LC[st][0][0][:, :], xrb[:, st, :]))
                pairs.append((LC[st][1][0][:, :], xib[:, st, :]))
                rd += [LC[st][0][1], LC[st][1][1]]
            mm_group(S, pY[:, :], rY, pairs, reads=rd + [r_xrb, r_xib])
            act(ypre[:, :], pY[:, :], AF.Copy, [rY], [r_ypre])
            stt(yss[:, :], u32[:, ct, :], d_t[:, ct:ct + 1], ypre[:, :], ALU.mult, ALU.add, [r_u32, r_d, r_ypre], [r_yss])
            gelu_tanh_mul(cx, yss[:, :], r_yss, None, None, yout[:, ct, :], r_yout, (t1, r_t1, t2, r_t2))
        store_T(cx, y_o, tok0, yout, r_yout, 2)
    S.finish("sp")
    S.emit()
    return cx.nc


def s5_inputs(W, half, cast):
    w_in = W["od_w_in"][0]
    g0 = 16 * half
    gs = slice(g0, g0 + 16)
    c = lambda a: np.ascontiguousarray(cast(a))

    def per_state(a):
        return np.ascontiguousarray(a.reshape(8, 128).T)

    lr = per_state(W["s5_a_re"][0][gs])
    li = per_state(W["s5_a_im"][0][gs])
    ldt = per_state(np.repeat(W["s5_log_dt"][0][gs][:, None], 64, 1))
    bre = np.ascontiguousarray(W["s5_b_re"][0][gs].reshape(8, 128, 16).transpose(1, 0, 2).reshape(128, 128))
    bim = np.ascontiguousarray(W["s5_b_im"][0][gs].reshape(8, 128, 16).transpose(1, 0, 2).reshape(128, 128))
    lc_re = np.zeros((8, 128, 128), np.float32)
    lc_im = np.zeros((8, 128, 128), np.float32)
    for st in range(8):
        for j in range(2):
            g = 2 * st + j
            gl = g % 8
            lc_re[st, j * 64:(j + 1) * 64, gl * 16:(gl + 1) * 16] = W["s5_c_re"][0][g0 + g].T
            lc_im[st, j * 64:(j + 1) * 64, gl * 16:(gl + 1) * 16] = W["s5_c_im"][0][g0 + g].T
    return dict(w_u=c(w_in[:, 1024 + 256 * half:1024 + 256 * half + 256]), lr=lr, li=li, ldt=ldt, b_re=bre, b_im=bim,
                lc_re=lc_re.reshape(1024, 128), lc_im=lc_im.reshape(1024, 128),
                d=np.ascontiguousarray(W["s5_d"][0][gs].reshape(256)))


BIG = ["ffn_a_w1", "ffn_a_w3", "ffn_a_w2", "ffn_b_w1", "ffn_b_w3", "ffn_b_w2", "ple_w_gate", "ple_w_up",
       "ev_w_in", "mla_w_q_up", "mla_w_kv_up", "gla_w_gate_up", "ev_w_out", "od_w_in", "rg_w_a", "rg_w_i",
       "s5_w_glu", "od_w_out"]


def _run(nc, in_maps):
    res = run_bass_kernel_spmd(nc, in_maps, core_ids=list(range(NCORES)))
    return res.results


def _ident(a):
    return a


def kernel(**inp):
    inp = {k: np.asarray(v) for k, v in inp.items()}
    x = inp["x"]
    p = inp["p"]
    B = x.shape[0]
    flat = {n: np.ascontiguousarray(inp[n]).reshape(-1) for n in BIG}
    sizes = [flat[n].size // NCORES for n in BIG]
    nc0 = build_cast(sizes)
    maps = [{f"w{i}": flat[n][c * sz:(c + 1) * sz].reshape(-1, 512) for i, (n, sz) in enumerate(zip(BIG, sizes))}
            for c in range(NCORES)]
    r0 = _run(nc0, maps)
    W = {k: v for k, v in inp.items() if k not in ("x", "p")}
    for i, n in enumerate(BIG):
        W[n] = np.concatenate([np.asarray(r0[c][f"o{i}"]).reshape(-1) for c in range(NCORES)]).reshape(inp[n].shape)

    def tok(c):
        b, half = divmod(c, 2)
        return b, slice(half * TOK, (half + 1) * TOK)

    cc = np.ascontiguousarray
    maps = []
    for c in range(NCORES):
        b, ts_ = tok(c)
        maps.append(dict(x=cc(x[b, ts_]), w1=W["ffn_a_w1"][0], w3=W["ffn_a_w3"][0], w2=W["ffn_a_w2"][0],
                         gA=W["ffn_a_norm"][0], gM=W["mix_norm"][0]))
    r1 = _run(build_L1(), maps)
    hT = [np.asarray(r1[c]["hT"]) for c in range(NCORES)]
    xnT = [np.asarray(r1[c]["xnT"]) for c in range(NCORES)]

    def full_seq(parts):
        return [cc(np.concatenate([parts[2 * b], parts[2 * b + 1]], axis=1)) for b in range(B)]

    def mixer(build_a, in_a, build_b, in_b, xn_full):
        ma, mb = [], []
        for c in range(NCORES):
            b, half = divmod(c, 2)
            da = in_a(W, half, _ident)
            da["xnT"] = xn_full[b]
            ma.append(da)
            db = in_b(W, half, _ident)
            db["xnT"] = xn_full[b]
            mb.append(db)
        ra = _run(build_a(), ma)
        rb = _run(build_b(), mb)
        ycat = []
        for b in range(B):
            ycat.append(np.concatenate([np.asarray(ra[2 * b]["yT"]), np.asarray(ra[2 * b + 1]["yT"]),
                                        np.asarray(rb[2 * b]["yT"]), np.asarray(rb[2 * b + 1]["yT"])], axis=0))
        return ycat

    ycat = mixer(build_mla, mla_inputs, build_gla, gla_inputs, full_seq(xnT))
    maps = []
    for c in range(NCORES):
        b, ts_ = tok(c)
        maps.append(dict(hT_in=hT[c], yT_in=cc(ycat[b][:, ts_]), p=cc(p[0, b, ts_]), w_out=W["ev_w_out"][0],
                         w1=W["ffn_b_w1"][0], w3=W["ffn_b_w3"][0], w2=W["ffn_b_w2"][0], gB=W["ffn_b_norm"][0],
                         gP=W["ple_norm"][0], w_gate=W["ple_w_gate"][0], w_up=W["ple_w_up"][0],
                         n_w1=W["ffn_a_w1"][1], n_w3=W["ffn_a_w3"][1], n_w2=W["ffn_a_w2"][1],
                         gA=W["ffn_a_norm"][1], gM=W["mix_norm"][1]))
    r3 = _run(build_row(False), maps)
    hT = [np.asarray(r3[c]["hT"]) for c in range(NCORES)]
    xnT = [np.asarray(r3[c]["xnT"]) for c in range(NCORES)]
    ycat = mixer(build_rglru, rglru_inputs, build_s5, s5_inputs, full_seq(xnT))
    maps = []
    for c in range(NCORES):
        b, ts_ = tok(c)
        maps.append(dict(hT_in=hT[c], yT_in=cc(ycat[b][:, ts_]), p=cc(p[1, b, ts_]), w_out=W["od_w_out"][0],
                         w1=W["ffn_b_w1"][1], w3=W["ffn_b_w3"][1], w2=W["ffn_b_w2"][1], gB=W["ffn_b_norm"][1],
                         gP=W["ple_norm"][1], w_gate=W["ple_w_gate"][1], w_up=W["ple_w_up"][1],
                         w_glu=W["s5_w_glu"][0], b_glu=W["s5_b_glu"][0], gF=W["final_norm"]))
    r5 = _run(build_row(True), maps)
    out = np.empty(x.shape, np.float32)
    for c in range(NCORES):
        b, ts_ = tok(c)
        out[b, ts_] = np.asarray(r5[c]["out"])
    return out
```
